# Optimizing a Trainium2 kernel written in Bass

```python
import jax, jax.numpy as jnp
from jax import lax
import numpy as np

D_MODEL = 2048
BATCH = 4
SEQ = 4096
DEPTH = 4

N_MIXERS = 3
N_HEADS = 16
HEAD_DIM = D_MODEL // N_HEADS
Q_BLOCK = 128
POOL_WINDOWS = (2, 4, 8, 16)
N_POOL_GROUPS = len(POOL_WINDOWS)
POOL_GROUP = D_MODEL // N_POOL_GROUPS
CONV_W = 3
D_FF = ((8 * D_MODEL // 3 + 255) // 256) * 256
EPS = 1e-6
N_SB = (DEPTH + 2) // 3
N_POOL = (DEPTH + 1) // 3
N_CONV = DEPTH // 3

kernel_name = "hybrid_stickbreak_pool_shortconv_trunk"


def rmsnorm(x, g):
    xf = x.astype(jnp.float32)
    y = xf * lax.rsqrt(jnp.mean(xf * xf, axis=-1, keepdims=True) + EPS)
    return (y * g.astype(jnp.float32)).astype(x.dtype)


def head_rmsnorm(x, g):
    xf = x.astype(jnp.float32)
    return xf * lax.rsqrt(jnp.mean(xf * xf, axis=-1, keepdims=True) + EPS) * g.astype(jnp.float32)


def stick_breaking_attention(h, w_qkv, g_q, g_k, w_o):
    B, S, D = h.shape
    qkv = h @ w_qkv
    q, k, v = jnp.split(qkv, 3, axis=-1)
    q = head_rmsnorm(q.reshape(B, S, N_HEADS, HEAD_DIM), g_q)
    k = head_rmsnorm(k.reshape(B, S, N_HEADS, HEAD_DIM), g_k)
    v = v.reshape(B, S, N_HEADS, HEAD_DIM).astype(jnp.float32)
    scale = HEAD_DIM ** -0.5
    outs = []
    for blk in range(S // Q_BLOCK):
        t0 = blk * Q_BLOCK
        kl = t0 + Q_BLOCK
        z = jnp.einsum('bthd,bshd->bhts', q[:, t0:kl], k[:, :kl]) * scale
        t_idx = t0 + jnp.arange(Q_BLOCK)[:, None]
        s_idx = jnp.arange(kl)[None, :]
        mask = s_idx < t_idx
        log_keep = jnp.where(mask, jax.nn.log_sigmoid(-z), 0.0)
        after = lax.cumsum(log_keep, axis=3, reverse=True) - log_keep
        log_a = jax.nn.log_sigmoid(z) + after
        a = jnp.where(mask, jnp.exp(log_a), 0.0)
        o = jnp.einsum('bhts,bshd->bthd', a, v[:, :kl])
        outs.append(o.astype(h.dtype))
    o = jnp.concatenate(outs, axis=1).reshape(B, S, D)
    return o @ w_o


def multiscale_pool(h, w_pool, scale):
    B, S, D = h.shape
    hg = h.astype(jnp.float32).reshape(B, S, N_POOL_GROUPS, POOL_GROUP)
    c = jnp.cumsum(hg, axis=1)
    pos1 = jnp.arange(1, S + 1)
    pieces = []
    for g, w in enumerate(POOL_WINDOWS):
        cg = c[:, :, g]
        lag = jnp.pad(cg, ((0, 0), (w, 0), (0, 0)))[:, :S]
        cnt = jnp.minimum(pos1, w).astype(jnp.float32)[None, :, None]
        pieces.append((cg - lag) / cnt - hg[:, :, g])
    p = jnp.stack(pieces, axis=2).astype(h.dtype)
    y = jnp.einsum('bsgc,gcd->bsgd', p, w_pool).reshape(B, S, D)
    return y * scale


def short_gated_conv(h, w_in, w_conv, w_out):
    D = h.shape[-1]
    bcx = h @ w_in
    b, c, u = jnp.split(bcx, 3, axis=-1)
    g = c * u
    y = lax.conv_general_dilated(
        g, w_conv[:, None, :].astype(g.dtype), window_strides=(1,),
        padding=[(CONV_W - 1, 0)], dimension_numbers=('NWC', 'WIO', 'NWC'),
        feature_group_count=D)
    return (b * y) @ w_out


def swiglu(h, w_gate, w_up, w_down):
    return (jax.nn.silu(h @ w_gate) * (h @ w_up)) @ w_down


def setup_inputs(seed: int = 0) -> dict:
    key = jax.random.key(seed)
    ks = jax.random.split(key, 16)
    f32 = jnp.float32
    D, F = D_MODEL, D_FF
    def nrm(k, shape, s):
        return jax.random.normal(k, shape, f32) * s
    return {
        "x": jax.random.normal(ks[0], (BATCH, SEQ, D), f32),
        "norm_mix_g": 1.0 + nrm(ks[1], (DEPTH, D), 0.02),
        "norm_ffn_g": 1.0 + nrm(ks[2], (DEPTH, D), 0.02),
        "sb_w_qkv": nrm(ks[3], (N_SB, D, 3 * D), D ** -0.5),
        "sb_g_q": 1.0 + nrm(ks[4], (N_SB, HEAD_DIM), 0.02),
        "sb_g_k": 1.0 + nrm(ks[5], (N_SB, HEAD_DIM), 0.02),
        "sb_w_o": nrm(ks[6], (N_SB, D, D), D ** -0.5),
        "pool_w": nrm(ks[7], (N_POOL, N_POOL_GROUPS, POOL_GROUP, POOL_GROUP), POOL_GROUP ** -0.5),
        "pool_scale": 1.0 + nrm(ks[8], (N_POOL, D), 0.02),
        "conv_w_in": nrm(ks[9], (N_CONV, D, 3 * D), D ** -0.5),
        "conv_w": nrm(ks[10], (N_CONV, CONV_W, D), CONV_W ** -0.5),
        "conv_w_out": nrm(ks[11], (N_CONV, D, D), D ** -0.5),
        "ffn_w_gate": nrm(ks[12], (DEPTH, D, F), D ** -0.5),
        "ffn_w_up": nrm(ks[13], (DEPTH, D, F), D ** -0.5),
        "ffn_w_down": nrm(ks[14], (DEPTH, F, D), F ** -0.5),
    }


def reference(x, norm_mix_g, norm_ffn_g, sb_w_qkv, sb_g_q, sb_g_k, sb_w_o,
              pool_w, pool_scale, conv_w_in, conv_w, conv_w_out,
              ffn_w_gate, ffn_w_up, ffn_w_down):
    for i in range(DEPTH):
        kind, j = i % N_MIXERS, i // N_MIXERS
        h = rmsnorm(x, norm_mix_g[i])
        if kind == 0:
            x = x + stick_breaking_attention(h, sb_w_qkv[j], sb_g_q[j], sb_g_k[j], sb_w_o[j])
        elif kind == 1:
            x = x + multiscale_pool(h, pool_w[j], pool_scale[j])
        else:
            x = x + short_gated_conv(h, conv_w_in[j], conv_w[j], conv_w_out[j])
        h = rmsnorm(x, norm_ffn_g[i])
        x = x + swiglu(h, ffn_w_gate[i], ffn_w_up[i], ffn_w_down[i])
    return x
```

```python
import numpy as np
import concourse.bass as bass
import concourse.mybir as mybir
from concourse.bass_utils import run_bass_kernel_spmd

F32 = mybir.dt.float32
BF16 = mybir.dt.bfloat16
AF = mybir.ActivationFunctionType
ALU = mybir.AluOpType

D = 2048
DC = 16
FF = 5632
FC = 44
NH = 16
TT = 512
EPS = 1e-6
DEPTH = 4
POOL_W = (2, 4, 8, 16)
N_CORES = 8

V_NMIX = 0
V_NFFN = 64
V_PSCALE = 128
V_CONVW = 144
V_GQ = 192
V_GK = 194
V_INVCNT = 196
NV = 260


class Prog:
    ENGS = ('pe', 'act', 'dve', 'pool', 'sync')

    def __init__(self, nc):
        self.nc = nc
        self.q = {e: [] for e in self.ENGS}
        self.semnames = []
        self.count = {}
        self.waited = {e: {} for e in self.ENGS}
        self.res = {}

    def _wait(self, eng, sem, val):
        if self.waited[eng].get(sem, 0) >= val:
            return
        self.waited[eng][sem] = val
        self.q[eng].append(('wait', sem, val))

    def op(self, eng, fn, reads=(), writes=(), sem=None, inc=1):
        for k in reads:
            r = self.res.get(k)
            if r and r[0]:
                self._wait(eng, *r[0])
        for k in writes:
            r = self.res.get(k)
            if r:
                if r[0]:
                    self._wait(eng, *r[0])
                for s, v in r[1].items():
                    self._wait(eng, s, v)
        s = sem if sem else 'E_' + eng
        if s not in self.count:
            self.count[s] = 0
            self.semnames.append(s)
        self.count[s] += inc
        tok = (s, self.count[s])
        self.q[eng].append(('op', fn, s, inc))
        for k in reads:
            r = self.res.setdefault(k, [None, {}])
            r[1][s] = max(r[1].get(s, 0), tok[1])
        for k in writes:
            self.res[k] = [tok, {}]
        return tok

    def dma(self, queue, out, in_, sem, reads=(), writes=()):
        return self.op(queue, lambda e: e.dma_start(out=out, in_=in_), reads, writes, sem=sem, inc=16)

    def wait_tok(self, eng, tok):
        self._wait(eng, *tok)

    def barrier(self):
        for s, v in list(self.count.items()):
            if s.startswith('S_c_') or s.startswith('S_rA') or s.startswith('S_rB'):
                continue
            for eng in self.ENGS:
                self._wait(eng, s, v)

    def replay(self):
        nc = self.nc
        sems = {n: nc.alloc_semaphore(n) for n in self.semnames}
        engmap = {'pe': 'tensor', 'act': 'scalar', 'dve': 'vector', 'pool': 'gpsimd', 'sync': 'sync'}
        with nc.Block() as block:
            for eng in self.ENGS:
                items = self.q[eng]

                def body(e, items=items):
                    for it in items:
                        if it[0] == 'wait':
                            e.wait_ge(sems[it[1]], it[2])
                        else:
                            it[1](e).then_inc(sems[it[2]], it[3])

                getattr(block, engmap[eng])(body)


class Ring:
    def __init__(self, P, name, ap, nslots, slot_elems):
        self.P = P
        self.name = name
        self.n = nslots
        self.i = 0
        self.slots = [ap[:, s * slot_elems:(s + 1) * slot_elems] for s in range(nslots)]

    def load(self, src, kc, dep):
        s = self.i % self.n
        self.i += 1
        key = (self.name, s)
        view = self.slots[s][:, 0:kc * 128].rearrange("p (k j) -> p k j", j=128)
        self.P.dma('sync', view, src, 'S_%s%d' % (self.name, s), reads=(dep,), writes=(key,))
        return key, view


def build(T, layers, NSEQ=1, with_ffn=True):
    nc = bass.Bass("TRN2", target_bir_lowering=False)
    NTT = T // TT
    NTB = T // 128
    S = T // NSEQ
    SNT = S // TT
    P = Prog(nc)

    x_in = nc.dram_tensor("x", [T, D], F32, kind="ExternalInput").ap()
    vecs_in = nc.dram_tensor("vecs", [128, NV], F32, kind="ExternalInput").ap()
    out = nc.dram_tensor("out", [T, D], F32, kind="ExternalOutput").ap()
    xres = nc.dram_tensor("xres", [DC, 128, T], F32).ap()
    xres_v = xres.rearrange("c p t -> p c t")

    kinds = [k for k, _ in layers]
    W = {}
    WB = {}

    def decl_w(name, K, M):
        W[name] = nc.dram_tensor(name, [K, M], F32, kind="ExternalInput").ap()
        WB[name] = nc.dram_tensor("b_" + name, [M // 128, 128, K // 128, 128], BF16).ap()

    for kind, li in layers:
        if kind == 'sb':
            decl_w("wqkv%d" % li, D, 3 * D)
            decl_w("wo%d" % li, D, D)
        elif kind == 'pool':
            W["wpool%d" % li] = nc.dram_tensor("wpool%d" % li, [4, 512, 512], F32, kind="ExternalInput").ap()
            WB["wpool%d" % li] = nc.dram_tensor("b_wpool%d" % li, [16, 128, 4, 128], BF16).ap()
        elif kind == 'conv':
            decl_w("wcin%d" % li, D, 3 * D)
            decl_w("wcout%d" % li, D, D)
        if with_ffn:
            decl_w("wg%d" % li, D, FF)
            decl_w("wu%d" % li, D, FF)
            decl_w("wd%d" % li, FF, D)
    if 'sb' in kinds:
        qT_d = nc.dram_tensor("qT_d", [NH, 128, T], BF16).ap()
        kT_d = nc.dram_tensor("kT_d", [NH, 128, T], BF16).ap()
        v_d = nc.dram_tensor("v_d", [NTB, 128, D], BF16).ap()
        oT_d = nc.dram_tensor("oT_d", [NH, 128, T], BF16).ap()

    def sb(name, shape, dt):
        return nc.alloc_sbuf_tensor(name, shape, dt).ap()

    vecs = sb("vecs_sb", [128, NV], F32)
    gqs = sb("gqs", [128, 2], F32)
    ident_f = sb("ident_f", [128, 128], F32)
    ident_b = sb("ident_b", [128, 128], BF16)
    ones_b = sb("ones_b", [128, 128], BF16)
    nones_b = sb("nones_b", [128, 128], BF16)
    ntri_b = sb("ntri_b", [128, 128], BF16)
    mask_b = sb("mask_b", [128, 128], BF16)
    tmpc_f = sb("tmpc_f", [128, 128], F32)
    ringA_t = sb("ringA", [128, 8 * 2048], BF16)
    ringB_t = sb("ringB", [128, 3 * FC * 128], BF16)
    ringA = Ring(P, "rA", ringA_t, 8, 2048)
    ringB = Ring(P, "rB", ringB_t, 3, FC * 128)
    xT = sb("xT", [128, DC, TT], F32)
    hT = sb("hT", [128, DC, 16 + TT], BF16)
    big = sb("big", [128, FC * TT], BF16)
    rstd = sb("rstd", [128, 2, TT], F32)
    lnt = sb("lnt", [128, 2, TT], F32)
    sgt = sb("sgt", [128, 2, TT], BF16)
    misc = sb("misc", [128, 12 * 1024], BF16)
    ps = [nc.alloc_psum_tensor("ps%d" % i, [128, 512], F32).ap() for i in range(8)]
    PSK = [('ps', i) for i in range(8)]

    hTd = hT[:, :, 16:16 + TT]
    act = big.rearrange("p (f t) -> p f t", t=TT)

    P.dma('sync', vecs, vecs_in, 'S_vecs', writes=('vecs',))
    P.op('pool', lambda e: e.memset(tmpc_f, 1.0), writes=('tmpc',))
    P.op('pool', lambda e: e.affine_select(out=ident_f, in_=tmpc_f, pattern=[[-1, 128]], compare_op=ALU.is_equal,
                                           fill=0.0, base=0, channel_multiplier=1), reads=('tmpc',), writes=('ident_f',))
    P.op('pool', lambda e: e.tensor_copy(out=ident_b, in_=ident_f), reads=('ident_f',), writes=('ident_b',))
    P.op('pool', lambda e: e.memset(ones_b, 1.0), writes=('ones_b',))
    P.op('pool', lambda e: e.memset(nones_b, -1.0), writes=('nones_b',))
    P.op('pool', lambda e: e.affine_select(out=mask_b, in_=ones_b, pattern=[[1, 128]], compare_op=ALU.is_gt,
                                           fill=0.0, base=0, channel_multiplier=-1), reads=('ones_b',), writes=('mask_b',))
    P.op('pool', lambda e: e.affine_select(out=ntri_b, in_=nones_b, pattern=[[-1, 128]], compare_op=ALU.is_ge,
                                           fill=0.0, base=0, channel_multiplier=1), reads=('nones_b',), writes=('ntri_b',))
    P.op('dve', lambda e: e.tensor_scalar(out=gqs, in0=vecs[:, V_GQ:V_GQ + 2], scalar1=float(128 ** -0.5), scalar2=None,
                                          op0=ALU.mult), reads=('vecs',), writes=('gqs',))
    CONSTS = ('vecs', 'gqs', 'ident_f', 'ident_b', 'ones_b', 'nones_b', 'ntri_b', 'mask_b')

    def cast_w(name):
        w = W[name]
        wb = WB[name]
        key = ('wb', name)
        if name.startswith('wpool'):
            for g in range(4):
                for mo in range(4):
                    src = w[g][:, mo * 128:(mo + 1) * 128].rearrange("(kc p) j -> p kc j", p=128)
                    P.dma('pool', wb[g * 4 + mo], src, 'S_c_' + name)
        else:
            MC = wb.shape[0]
            for m in range(MC):
                src = w[:, m * 128:(m + 1) * 128].rearrange("(kc p) j -> p kc j", p=128)
                P.dma('pool', wb[m], src, 'S_c_' + name)
        P.res[key] = [('S_c_' + name, P.count['S_c_' + name]), {}]
        return key

    cast_keys = {}

    def need_w(name):
        if name not in cast_keys:
            cast_keys[name] = cast_w(name)
        return cast_keys[name]

    def layer_weights(kind, li):
        names = []
        if kind == 'sb':
            names += ["wqkv%d" % li, "wo%d" % li]
        elif kind == 'pool':
            names += ["wpool%d" % li]
        elif kind == 'conv':
            names += ["wcin%d" % li, "wcout%d" % li]
        if with_ffn:
            names += ["wg%d" % li, "wu%d" % li, "wd%d" % li]
        return names

    pending = []

    def tile_hook():
        if pending:
            need_w(pending.pop(0))

    def rmsnorm(gcol):
        sq = act
        for g4 in range(4):
            c0 = g4 * 4
            P.op('act', lambda e, c0=c0: e.activation(out=sq[:, c0:c0 + 4, :], in_=xT[:, c0:c0 + 4, :], func=AF.Square),
                 reads=[('xT', c) for c in range(c0, c0 + 4)], writes=[('act', c) for c in range(c0, c0 + 4)])

        def mm(e):
            ins = None
            for c in range(DC):
                ins = e.matmul(ps[6], lhsT=ones_b, rhs=sq[:, c, :], start=(c == 0), stop=(c == DC - 1))
            return ins
        P.op('pe', mm, reads=[('act', c) for c in range(DC)] + ['ones_b'], writes=[PSK[6]])
        P.op('act', lambda e: e.activation(out=lnt[:, 0, :], in_=ps[6], func=AF.Ln, scale=1.0 / D, bias=EPS),
             reads=[PSK[6]], writes=[('lnt', 0)])
        P.op('act', lambda e: e.activation(out=rstd[:, 0, :], in_=lnt[:, 0, :], func=AF.Exp, scale=-0.5),
             reads=[('lnt', 0)], writes=[('rstd', 0)])
        for c in range(DC):
            P.op('dve', lambda e, c=c: e.scalar_tensor_tensor(out=hTd[:, c, :], in0=xT[:, c, :],
                                                               scalar=vecs[:, gcol + c:gcol + c + 1], in1=rstd[:, 0, :],
                                                               op0=ALU.mult, op1=ALU.mult),
                 reads=[('xT', c), ('rstd', 0), 'vecs'], writes=[('hT', c)])

    def load_x(tt):
        P.dma('sync', xT, xres_v[:, :, tt * TT:(tt + 1) * TT], 'S_xld',
              reads=[('xres', tt)], writes=[('xT', c) for c in range(DC)])

    def store_x(tt):
        P.dma('act', xres_v[:, :, tt * TT:(tt + 1) * TT], xT, 'S_xst',
              reads=[('xT', c) for c in range(DC)], writes=[('xres', tt)])

    def proj_chunk(pbank, wkey, wview, kc_n, rhs_fn, rhs_keys):
        def mm(e):
            ins = None
            for k in range(kc_n):
                ins = e.matmul(ps[pbank], lhsT=wview[:, k, :], rhs=rhs_fn(k), start=(k == 0), stop=(k == kc_n - 1))
            return ins
        return P.op('pe', mm, reads=[wkey] + list(rhs_keys), writes=[PSK[pbank]])

    def ffn(li):
        if not with_ffn:
            return
        gcol = V_NFFN + li * 16
        rmsnorm(gcol)
        kg, ku, kd = need_w("wg%d" % li), need_w("wu%d" % li), need_w("wd%d" % li)
        wbg, wbu, wbd = WB["wg%d" % li], WB["wu%d" % li], WB["wd%d" % li]
        hkeys = [('hT', c) for c in range(DC)]
        for f in range(FC):
            b = f % 2
            wk, wv = ringA.load(wbg[f], DC, kg)
            proj_chunk(0 + b, wk, wv, DC, lambda k: hTd[:, k, :], hkeys)
            wk, wv = ringA.load(wbu[f], DC, ku)
            proj_chunk(2 + b, wk, wv, DC, lambda k: hTd[:, k, :], hkeys)
            P.op('act', lambda e, b=b: e.activation(out=sgt[:, b, :], in_=ps[0 + b], func=AF.Silu),
                 reads=[PSK[0 + b]], writes=[('sgt', b)])
            P.op('dve', lambda e, b=b, f=f: e.tensor_tensor(out=act[:, f, :], in0=sgt[:, b, :], in1=ps[2 + b], op=ALU.mult),
                 reads=[('sgt', b), PSK[2 + b]], writes=[('act', f)])
        akeys = [('act', f) for f in range(FC)]
        for m in range(DC):
            b = 4 + (m % 2)
            wk, wv = ringB.load(wbd[m], FC, kd)
            proj_chunk(b, wk, wv, FC, lambda k: act[:, k, :], akeys)
            P.op('dve', lambda e, b=b, m=m: e.tensor_tensor(out=xT[:, m, :], in0=xT[:, m, :], in1=ps[b], op=ALU.add),
                 reads=[PSK[b]], writes=[('xT', m)])

    def transpose_in():
        xin = misc.bitcast(F32).rearrange("p (b n) -> p b n", b=2)[:, :, 0:D]
        stg = big.bitcast(F32)[:, 0:2 * D].rearrange("p (b c t) -> p b c t", b=2, c=DC)
        for tb in range(NTB):
            b = tb % 2
            P.dma('sync', xin[:, b, :], x_in[tb * 128:(tb + 1) * 128, :], 'S_xin%d' % b, writes=[('xin', b)])
            for g in range(4):
                def tr(e, g=g, b=b):
                    ins = None
                    for j in range(4):
                        c = g * 4 + j
                        ins = e.transpose(out=ps[g][:, j * 128:(j + 1) * 128], in_=xin[:, b, c * 128:(c + 1) * 128],
                                          identity=ident_f)
                    return ins
                P.op('pe', tr, reads=[('xin', b), 'ident_f'], writes=[PSK[g]])
                eng = 'act' if g % 2 == 0 else 'dve'
                if eng == 'act':
                    P.op('act', lambda e, g=g, b=b: e.activation(out=stg[:, b, g * 4:(g + 1) * 4, :],
                                                                 in_=ps[g].rearrange("p (c t) -> p c t", c=4), func=AF.Copy),
                         reads=[PSK[g]], writes=[('stg', b, g)])
                else:
                    P.op('dve', lambda e, g=g, b=b: e.tensor_copy(out=stg[:, b, g * 4:(g + 1) * 4, :],
                                                                  in_=ps[g].rearrange("p (c t) -> p c t", c=4)),
                         reads=[PSK[g]], writes=[('stg', b, g)])
            P.dma('act', xres_v[:, :, tb * 128:(tb + 1) * 128], stg[:, b], 'S_stg%d' % b,
                  reads=[('stg', b, g) for g in range(4)], writes=[('xresb', tb)])
        for tt in range(NTT):
            toks = [P.res[('xresb', tt * 4 + j)][0] for j in range(4)]
            P.res[('xres', tt)] = [None, {}]
            for s, v in toks:
                P.res[('xres', tt)][1][s] = max(P.res[('xres', tt)][1].get(s, 0), v)

    def xres_read_deps(eng, tt):
        r = P.res.get(('xres', tt))
        if r:
            for s, v in r[1].items():
                P._wait(eng, s, v)

    def transpose_out():
        xo = misc.bitcast(F32).rearrange("p (b n) -> p b n", b=2)[:, :, 0:D]
        for tt in range(NTT):
            xres_read_deps('sync', tt)
            load_x(tt)
            for j in range(4):
                tb = tt * 4 + j
                b = tb % 2
                for g in range(4):
                    def tr(e, g=g, j=j):
                        ins = None
                        for i in range(4):
                            c = g * 4 + i
                            ins = e.transpose(out=ps[g][:, i * 128:(i + 1) * 128], in_=xT[:, c, j * 128:(j + 1) * 128],
                                              identity=ident_f)
                        return ins
                    P.op('pe', tr, reads=[('xT', c) for c in range(g * 4, g * 4 + 4)] + ['ident_f'], writes=[PSK[g]])
                    if g % 2 == 0:
                        P.op('act', lambda e, g=g, b=b: e.activation(out=xo[:, b, g * 512:(g + 1) * 512], in_=ps[g], func=AF.Copy),
                             reads=[PSK[g]], writes=[('xo', b, g)])
                    else:
                        P.op('dve', lambda e, g=g, b=b: e.tensor_copy(out=xo[:, b, g * 512:(g + 1) * 512], in_=ps[g]),
                             reads=[PSK[g]], writes=[('xo', b, g)])
                P.dma('act', out[tb * 128:(tb + 1) * 128, :], xo[:, b, :], 'S_out%d' % b,
                      reads=[('xo', b, g) for g in range(4)], writes=[('outb', b)])
        for b in range(2):
            r = P.res.get(('outb', b))
            if r and r[0]:
                P.wait_tok('act', r[0])
                P.wait_tok('pool', r[0])
                P.wait_tok('sync', r[0])

    def sb_phase_a(li, j):
        kq = need_w("wqkv%d" % li)
        wb = WB["wqkv%d" % li]
        hkeys = [('hT', c) for c in range(DC)]
        qst = misc[:, 0:2 * TT].rearrange("p (b t) -> p b t", b=2)
        sqh = misc[:, 2 * TT:4 * TT].rearrange("p (b t) -> p b t", b=2)
        vst = misc[:, 4 * TT:4 * TT + 4 * D].rearrange("p (tb n) -> p tb n", tb=4)
        vT = sgt
        for tt in range(NTT):
            tile_hook()
            xres_read_deps('sync', tt)
            load_x(tt)
            rmsnorm(V_NMIX + li * 16)
            it = 0
            deferred = []
            for which in range(2):
                dst = qT_d if which == 0 else kT_d
                for hd in range(NH):
                    b = it % 2
                    it += 1
                    wk, wv = ringA.load(wb[which * NH + hd], DC, kq)
                    proj_chunk(0 + b, wk, wv, DC, lambda k: hTd[:, k, :], hkeys)
                    P.op('act', lambda e, b=b: e.activation(out=sqh[:, b, :], in_=ps[0 + b], func=AF.Square),
                         reads=[PSK[0 + b]], writes=[('sqh', b)])
                    P.op('pe', lambda e, b=b: e.matmul(ps[2 + b], lhsT=ones_b, rhs=sqh[:, b, :], start=True, stop=True),
                         reads=[('sqh', b), 'ones_b'], writes=[PSK[2 + b]])
                    P.op('act', lambda e, b=b: e.activation(out=lnt[:, b, :], in_=ps[2 + b], func=AF.Ln, scale=1.0 / 128, bias=EPS),
                         reads=[PSK[2 + b]], writes=[('lnt', b)])
                    P.op('act', lambda e, b=b: e.activation(out=rstd[:, b, :], in_=lnt[:, b, :], func=AF.Exp, scale=-0.5),
                         reads=[('lnt', b)], writes=[('rstd', b)])
                    gsc = gqs[:, j:j + 1] if which == 0 else vecs[:, V_GK + j:V_GK + j + 1]
                    P.op('dve', lambda e, b=b, gsc=gsc: e.scalar_tensor_tensor(out=qst[:, b, :], in0=ps[0 + b], scalar=gsc,
                                                                               in1=rstd[:, b, :], op0=ALU.mult, op1=ALU.mult),
                         reads=[PSK[0 + b], ('rstd', b), 'gqs', 'vecs'], writes=[('qst', b)])
                    if deferred:
                        deferred.pop()()
                    deferred.append(lambda dst=dst, hd=hd, tt=tt, b=b, which=which: P.dma(
                        'act', dst[hd][:, tt * TT:(tt + 1) * TT], qst[:, b, :], 'S_qst%d' % b,
                        reads=[('qst', b)], writes=[('qkd', which, hd, tt)]))
            if deferred:
                deferred.pop()()
            for hd in range(NH):
                b = hd % 2
                wk, wv = ringA.load(wb[2 * NH + hd], DC, kq)
                proj_chunk(4 + b, wk, wv, DC, lambda k: hTd[:, k, :], hkeys)
                P.op('act', lambda e, b=b: e.activation(out=vT[:, b, :], in_=ps[4 + b], func=AF.Copy),
                     reads=[PSK[4 + b]], writes=[('vT', b)])
                pvb = ps[6 + b].bitcast(BF16)

                def tr(e, b=b, pvb=pvb):
                    ins = None
                    for tb in range(4):
                        ins = e.transpose(out=pvb[:, tb * 128:(tb + 1) * 128], in_=vT[:, b, tb * 128:(tb + 1) * 128],
                                          identity=ident_b)
                    return ins
                P.op('pe', tr, reads=[('vT', b), 'ident_b'], writes=[PSK[6 + b]])
                P.op('dve', lambda e, hd=hd, pvb=pvb: e.tensor_copy(out=vst[:, :, hd * 128:(hd + 1) * 128],
                                                                    in_=pvb[:, 0:512].rearrange("p (tb d) -> p tb d", tb=4)),
                     reads=[PSK[6 + b]], writes=[('vst', hd)])
            P.dma('act', v_d[tt * 4:(tt + 1) * 4].rearrange("tb p n -> p tb n"), vst, 'S_vst',
                  reads=[('vst', hd) for hd in range(NH)], writes=[('vd', tt)])

    def sb_phase_b(li):
        HB = 3 * S
        assert HB <= 2 * DC * TT and HB <= FC * TT
        hb = [big[:, 0:HB], xT.rearrange("p c t -> p (c t)").bitcast(BF16)[:, 0:HB]]
        spb = misc[:, 0:3 * TT].rearrange("p (b t) -> p b t", b=3)
        atb = misc[:, 3 * TT:6 * TT].rearrange("p (b t) -> p b t", b=3)
        srun = misc[:, 6 * TT:9 * TT].rearrange("p (b t) -> p b t", b=3)
        ost = misc[:, 9 * TT:11 * TT].rearrange("p (b t) -> p b t", b=2)
        et = lnt
        steps = []
        for hd in range(NSEQ * NH):
            for qi in range(SNT):
                for kb in range(4 * qi + 3, -1, -1):
                    steps.append((hd, qi, kb))
        NS = len(steps)
        loaded = set()
        head_start = {}
        for ii, st in enumerate(steps):
            head_start.setdefault(st[0], ii)

        def load_head(hd):
            if hd in loaded or hd >= NSEQ * NH:
                return
            loaded.add(hd)
            sq_, h_ = hd // NH, hd % NH
            t0_ = sq_ * S
            hbuf = hb[hd % 2]
            key = ('hb', hd % 2)
            sem = 'S_hb%d' % (hd % 2)
            P.dma('sync', hbuf[:, 0:S], qT_d[h_][:, t0_:t0_ + S], sem, writes=[key])
            P.op('sync', lambda e: e.dma_start(out=hbuf[:, S:2 * S], in_=kT_d[h_][:, t0_:t0_ + S]), sem=sem, inc=16)
            tok = P.op('sync', lambda e: e.dma_start(
                out=hbuf[:, 2 * S:3 * S].rearrange("p (tb d) -> p tb d", d=128),
                in_=v_d[t0_ // 128:(t0_ + S) // 128, :, h_ * 128:(h_ + 1) * 128].rearrange("tb p d -> p tb d")),
                sem=sem, inc=16)
            P.res[key] = [tok, {}]

        def geom(st):
            hd, qi, kb = st
            m = kb - 4 * qi
            c0 = 128 * m if m > 0 else 0
            return hd, qi, kb, m, c0

        def stage1(i):
            hd, qi, kb, m, c0 = geom(steps[i])
            if i == 0:
                load_head(0)
            if i == head_start[hd] + 3:
                load_head(hd + 1)
            hbuf = hb[hd % 2]
            hk = ('hb', hd % 2)
            zb = i % 2
            N = TT - c0
            qs = hbuf[:, qi * TT + c0:(qi + 1) * TT]
            ks = hbuf[:, S + kb * 128:S + (kb + 1) * 128]
            P.op('pe', lambda e: e.matmul(ps[zb][:, 0:N], lhsT=ks, rhs=qs, start=True, stop=True),
                 reads=[hk], writes=[PSK[zb]])
            P.op('act', lambda e: e.activation(out=et[:, zb, 0:N], in_=ps[zb][:, 0:N], func=AF.Exp),
                 reads=[PSK[zb]], writes=[('et', zb)])
            sp = spb[:, i % 3, :]
            P.op('act', lambda e: e.activation(out=sp[:, 0:N], in_=et[:, zb, 0:N], func=AF.Ln, bias=1.0),
                 reads=[('et', zb)], writes=[('sp', i % 3)])
            if m >= 0:
                P.op('dve', lambda e: e.tensor_tensor(out=sp[:, 0:128], in0=sp[:, 0:128], in1=mask_b, op=ALU.mult),
                     reads=['mask_b'], writes=[('sp', i % 3)])
            first = (kb == 4 * qi + 3)
            sn = srun[:, i % 3, :]
            so = srun[:, (i + 2) % 3, :]
            if first:
                if c0 > 0:
                    P.op('dve', lambda e: e.memset(sn[:, 0:c0], 0.0), writes=[('srun', i % 3)])
                P.op('dve', lambda e: e.tensor_copy(out=sn[:, c0:TT], in_=sp[:, 0:N]),
                     reads=[('sp', i % 3)], writes=[('srun', i % 3)])
            else:
                if c0 > 0:
                    P.op('dve', lambda e: e.memset(sn[:, 0:c0], 0.0), writes=[('srun', i % 3)])
                P.op('dve', lambda e: e.tensor_tensor(out=sn[:, c0:TT], in0=so[:, c0:TT], in1=sp[:, 0:N], op=ALU.add),
                     reads=[('sp', i % 3), ('srun', (i + 2) % 3)], writes=[('srun', i % 3)])

        def stage2(i):
            hd, qi, kb, m, c0 = geom(steps[i])
            hbuf = hb[hd % 2]
            hk = ('hb', hd % 2)
            rb = 2 + i % 2
            N = TT - c0
            first = (kb == 4 * qi + 3)
            qs = hbuf[:, qi * TT + c0:(qi + 1) * TT]
            ks = hbuf[:, S + kb * 128:S + (kb + 1) * 128]
            sp = spb[:, i % 3, :]
            so = srun[:, (i + 2) % 3, :]

            def mm(e):
                e.matmul(ps[rb][:, 0:N], lhsT=ntri_b, rhs=sp[:, 0:N], start=True, stop=False)
                if not first:
                    e.matmul(ps[rb][:, 0:N], lhsT=nones_b, rhs=so[:, c0:TT], start=False, stop=False)
                return e.matmul(ps[rb][:, 0:N], lhsT=ks, rhs=qs, start=False, stop=True)
            rd = [hk, ('sp', i % 3), 'ntri_b', 'nones_b']
            if not first:
                rd.append(('srun', (i + 2) % 3))
            P.op('pe', mm, reads=rd, writes=[PSK[rb]])
            at = atb[:, i % 3, :]
            P.op('act', lambda e: e.activation(out=at[:, 0:N], in_=ps[rb][:, 0:N], func=AF.Exp),
                 reads=[PSK[rb]], writes=[('at', i % 3)])
            if m >= 0:
                P.op('dve', lambda e: e.tensor_tensor(out=at[:, 0:128], in0=at[:, 0:128], in1=mask_b, op=ALU.mult),
                     reads=['mask_b'], writes=[('at', i % 3)])

        def stage3(i):
            hd, qi, kb, m, c0 = geom(steps[i])
            hbuf = hb[hd % 2]
            hk = ('hb', hd % 2)
            ob = 4 + (hd * SNT + qi) % 2
            N = TT - c0
            first = (kb == 4 * qi + 3)
            at = atb[:, i % 3, :]
            vs = hbuf[:, 2 * S + kb * 128:2 * S + (kb + 1) * 128]
            P.op('pe', lambda e: e.matmul(ps[ob][:, c0:TT], lhsT=vs, rhs=at[:, 0:N], start=first, stop=(kb == 0),
                                          skip_group_check=True),
                 reads=[hk, ('at', i % 3)], writes=[PSK[ob]])
            if kb == 0:
                sbuf = (hd * SNT + qi) % 2
                gt0 = (hd // NH) * S + qi * TT
                P.op('dve', lambda e: e.tensor_copy(out=ost[:, sbuf, :], in_=ps[ob]),
                     reads=[PSK[ob]], writes=[('ost', sbuf)])
                P.dma('act', oT_d[hd % NH][:, gt0:gt0 + TT], ost[:, sbuf, :], 'S_ost%d' % sbuf,
                      reads=[('ost', sbuf)], writes=[('od', hd, qi)])

        for n in range(NS + 2):
            if n < NS:
                stage1(n)
            if 0 <= n - 1 < NS:
                stage2(n - 1)
            if 0 <= n - 2 < NS:
                stage3(n - 2)

    def sb_phase_c(li):
        ko = need_w("wo%d" % li)
        wb = WB["wo%d" % li]
        oT = act
        for tt in range(NTT):
            tile_hook()
            xres_read_deps('sync', tt)
            load_x(tt)
            P.dma('sync', oT[:, 0:NH, :], oT_d[:, :, tt * TT:(tt + 1) * TT].rearrange("h p t -> p h t"), 'S_old',
                  writes=[('act', h) for h in range(NH)])
            okeys = [('act', h) for h in range(NH)]
            for m in range(DC):
                b = m % 2
                wk, wv = ringA.load(wb[m], DC, ko)
                proj_chunk(b, wk, wv, DC, lambda k: oT[:, k, :], okeys)
                P.op('dve', lambda e, b=b, m=m: e.tensor_tensor(out=xT[:, m, :], in0=xT[:, m, :], in1=ps[b], op=ALU.add),
                     reads=[PSK[b]], writes=[('xT', m)])
            ffn(li)
            store_x(tt)

    def pool_layer(li):
        kp = need_w("wpool%d" % li)
        wb = WB["wpool%d" % li]
        pT = act
        U = 16 + TT
        wa = misc.bitcast(F32)[:, 0:4 * U].rearrange("p (c u) -> p c u", c=4)
        wb2 = misc.bitcast(F32)[:, 4 * U:8 * U].rearrange("p (c u) -> p c u", c=4)
        for tt in range(NTT):
            tile_hook()
            xres_read_deps('sync', tt)
            load_x(tt)
            if tt % SNT == 0:
                P.op('dve', lambda e: e.memset(hT[:, :, 0:16], 0.0), writes=[('hThalo',)])
            rmsnorm(V_NMIX + li * 16)
            for g in range(4):
                w = POOL_W[g]
                cs = slice(g * 4, g * 4 + 4)
                hk = [('hT', c) for c in range(g * 4, g * 4 + 4)] + [('hThalo',)]
                P.op('dve', lambda e, cs=cs: e.tensor_tensor(out=wa[:, :, 1:U], in0=hT[:, cs, 1:U], in1=hT[:, cs, 0:U - 1], op=ALU.add),
                     reads=hk, writes=['wa'])
                cur, oth, curk, othk = wa, wb2, 'wa', 'wb'
                sh = 2
                lo = 1
                while sh < w:
                    lo2 = lo + sh
                    P.op('dve', lambda e, cur=cur, oth=oth, lo2=lo2, sh=sh: e.tensor_tensor(
                        out=oth[:, :, lo2:U], in0=cur[:, :, lo2:U], in1=cur[:, :, lo2 - sh:U - sh], op=ALU.add),
                        reads=[curk], writes=[othk])
                    cur, oth, curk, othk = oth, cur, othk, curk
                    lo = lo2
                    sh *= 2
                P.op('dve', lambda e, cur=cur, cs=cs, w=w: e.scalar_tensor_tensor(
                    out=pT[:, cs, :], in0=cur[:, :, 16:U], scalar=1.0 / w, in1=hT[:, cs, 16:U], op0=ALU.mult, op1=ALU.subtract),
                    reads=[curk] + hk, writes=[('act', c) for c in range(g * 4, g * 4 + 4)])
                if tt % SNT == 0:
                    nfix = w - 1
                    for c in range(4):
                        P.op('dve', lambda e, cur=cur, c=c, g=g, nfix=nfix: e.tensor_tensor(
                            out=cur[:, c, 16:16 + nfix], in0=cur[:, c, 16:16 + nfix],
                            in1=vecs[:, V_INVCNT + g * 16:V_INVCNT + g * 16 + nfix], op=ALU.mult),
                            reads=[curk, 'vecs'], writes=[curk])
                        P.op('dve', lambda e, cur=cur, c=c, g=g, nfix=nfix: e.tensor_tensor(
                            out=pT[:, g * 4 + c, 0:nfix], in0=cur[:, c, 16:16 + nfix], in1=hT[:, g * 4 + c, 16:16 + nfix],
                            op=ALU.subtract),
                            reads=[curk] + hk, writes=[('act', g * 4 + c)])
                for mo in range(4):
                    c = g * 4 + mo
                    b = c % 2
                    wk, wv = ringA.load(wb[c], 4, kp)
                    proj_chunk(b, wk, wv, 4, lambda k, g=g: pT[:, g * 4 + k, :], [('act', g * 4 + k) for k in range(4)])
                    P.op('dve', lambda e, b=b, c=c: e.scalar_tensor_tensor(
                        out=xT[:, c, :], in0=ps[b], scalar=vecs[:, V_PSCALE + c:V_PSCALE + c + 1], in1=xT[:, c, :],
                        op0=ALU.mult, op1=ALU.add),
                        reads=[PSK[b], 'vecs'], writes=[('xT', c)])
            if (tt + 1) % SNT != 0:
                P.op('act', lambda e: e.activation(out=hT[:, :, 0:16], in_=hT[:, :, TT:TT + 16], func=AF.Copy),
                     reads=[('hT', c) for c in range(DC)], writes=[('hThalo',)])
            ffn(li)
            if tt + 1 < NTT:
                pass
            store_x(tt)

    def conv_layer(li):
        kci, kco = need_w("wcin%d" % li), need_w("wcout%d" % li)
        wbi, wbo = WB["wcin%d" % li], WB["wcout%d" % li]
        byT = act
        mf = misc.bitcast(F32)
        gb = mf[:, 0:2 * 516].rearrange("p (b u) -> p b u", b=2)
        ub = mf[:, 1032:1032 + 2 * TT].rearrange("p (b t) -> p b t", b=2)
        yb = mf[:, 2056:2056 + 2 * TT].rearrange("p (b t) -> p b t", b=2)
        gh = mf[:, 3080:3080 + 2 * DC].rearrange("p (c u) -> p c u", u=2)
        hkeys = [('hT', c) for c in range(DC)]
        for tt in range(NTT):
            tile_hook()
            xres_read_deps('sync', tt)
            load_x(tt)
            rmsnorm(V_NMIX + li * 16)
            if tt % SNT == 0:
                P.op('dve', lambda e: e.memset(gh, 0.0), writes=[('gh', c) for c in range(DC)])
            for m in range(DC):
                b = m % 2
                wk, wv = ringA.load(wbi[m], DC, kci)
                proj_chunk(0 + b, wk, wv, DC, lambda k: hTd[:, k, :], hkeys)
                wk, wv = ringA.load(wbi[DC + m], DC, kci)
                proj_chunk(2 + b, wk, wv, DC, lambda k: hTd[:, k, :], hkeys)
                wk, wv = ringA.load(wbi[2 * DC + m], DC, kci)
                proj_chunk(4 + b, wk, wv, DC, lambda k: hTd[:, k, :], hkeys)
                P.op('act', lambda e, b=b: e.activation(out=ub[:, b, :], in_=ps[4 + b], func=AF.Copy),
                     reads=[PSK[4 + b]], writes=[('ub', b)])
                P.op('dve', lambda e, b=b, m=m: e.tensor_copy(out=gb[:, b, 0:2], in_=gh[:, m, :]),
                     reads=[('gh', m)], writes=[('gb', b)])
                P.op('dve', lambda e, b=b: e.tensor_tensor(out=gb[:, b, 2:2 + TT], in0=ps[2 + b], in1=ub[:, b, :], op=ALU.mult),
                     reads=[PSK[2 + b], ('ub', b)], writes=[('gb', b)])
                P.op('dve', lambda e, b=b, m=m: e.tensor_copy(out=gh[:, m, :], in_=gb[:, b, TT:TT + 2]),
                     reads=[('gb', b)], writes=[('gh', m)])
                cw = lambda jj, m=m: vecs[:, V_CONVW + jj * 16 + m:V_CONVW + jj * 16 + m + 1]
                P.op('dve', lambda e, b=b, cw=cw: e.tensor_scalar(out=yb[:, b, :], in0=gb[:, b, 2:2 + TT], scalar1=cw(2), scalar2=None,
                                                                  op0=ALU.mult),
                     reads=[('gb', b), 'vecs'], writes=[('yb', b)])
                P.op('dve', lambda e, b=b, cw=cw: e.scalar_tensor_tensor(out=yb[:, b, :], in0=gb[:, b, 1:1 + TT], scalar=cw(1),
                                                                         in1=yb[:, b, :], op0=ALU.mult, op1=ALU.add),
                     reads=[('gb', b), 'vecs'], writes=[('yb', b)])
                P.op('dve', lambda e, b=b, cw=cw: e.scalar_tensor_tensor(out=yb[:, b, :], in0=gb[:, b, 0:TT], scalar=cw(0),
                                                                         in1=yb[:, b, :], op0=ALU.mult, op1=ALU.add),
                     reads=[('gb', b), 'vecs'], writes=[('yb', b)])
                P.op('dve', lambda e, b=b, m=m: e.tensor_tensor(out=byT[:, m, :], in0=ps[0 + b], in1=yb[:, b, :], op=ALU.mult),
                     reads=[PSK[0 + b], ('yb', b)], writes=[('act', m)])
            bkeys = [('act', m) for m in range(DC)]
            for mo in range(DC):
                b = 6 + mo % 2
                wk, wv = ringA.load(wbo[mo], DC, kco)
                proj_chunk(b, wk, wv, DC, lambda k: byT[:, k, :], bkeys)
                P.op('dve', lambda e, b=b, mo=mo: e.tensor_tensor(out=xT[:, mo, :], in0=xT[:, mo, :], in1=ps[b], op=ALU.add),
                     reads=[PSK[b]], writes=[('xT', mo)])
            ffn(li)
            store_x(tt)

    def ffn_only_layer(li):
        for tt in range(NTT):
            tile_hook()
            xres_read_deps('sync', tt)
            load_x(tt)
            ffn(li)
            store_x(tt)

    first = layer_weights(*layers[0])
    need_w(first[0])
    transpose_in()
    for nm in first[1:]:
        need_w(nm)
    P.barrier()
    for idx, (kind, li) in enumerate(layers):
        j = li // 3
        if idx + 1 < len(layers):
            pending.extend(layer_weights(*layers[idx + 1]))
        if kind == 'sb':
            sb_phase_a(li, j)
            P.barrier()
            sb_phase_b(li)
            P.barrier()
            sb_phase_c(li)
        elif kind == 'pool':
            pool_layer(li)
        elif kind == 'conv':
            conv_layer(li)
        elif kind == 'ffn':
            ffn_only_layer(li)
        while pending:
            need_w(pending.pop(0))
        P.barrier()
    transpose_out()
    P.replay()
    return nc


LAYERS = [('sb', 0), ('pool', 1), ('conv', 2), ('sb', 3)]


def make_vecs(norm_mix_g, norm_ffn_g, sb_g_q, sb_g_k, pool_scale, conv_w):
    v = np.zeros((128, NV), np.float32)
    fm = lambda a: np.asarray(a, np.float32).reshape(DC, 128).T
    for l in range(DEPTH):
        v[:, V_NMIX + l * 16:V_NMIX + (l + 1) * 16] = fm(norm_mix_g[l])
        v[:, V_NFFN + l * 16:V_NFFN + (l + 1) * 16] = fm(norm_ffn_g[l])
    v[:, V_PSCALE:V_PSCALE + 16] = fm(pool_scale[0])
    for jj in range(3):
        v[:, V_CONVW + jj * 16:V_CONVW + (jj + 1) * 16] = fm(conv_w[0][jj])
    for j in range(2):
        v[:, V_GQ + j] = np.asarray(sb_g_q[j], np.float32)
        v[:, V_GK + j] = np.asarray(sb_g_k[j], np.float32)
    for g, w in enumerate(POOL_W):
        v[:, V_INVCNT + g * 16:V_INVCNT + (g + 1) * 16] = (1.0 / np.minimum(np.arange(1, 17), w)).astype(np.float32)[None, :]
    return v


def make_in_maps(T, layers, x_shards, vecs, weights, with_ffn=True):
    maps = []
    for xs in x_shards:
        m = {"x": np.ascontiguousarray(xs, dtype=np.float32), "vecs": vecs}
        for kind, li in layers:
            j = li // 3
            if kind == 'sb':
                m["wqkv%d" % li] = weights["sb_w_qkv"][j]
                m["wo%d" % li] = weights["sb_w_o"][j]
            elif kind == 'pool':
                m["wpool%d" % li] = weights["pool_w"][0]
            elif kind == 'conv':
                m["wcin%d" % li] = weights["conv_w_in"][0]
                m["wcout%d" % li] = weights["conv_w_out"][0]
            if with_ffn:
                m["wg%d" % li] = weights["ffn_w_gate"][li]
                m["wu%d" % li] = weights["ffn_w_up"][li]
                m["wd%d" % li] = weights["ffn_w_down"][li]
        maps.append(m)
    return maps


def kernel(x, norm_mix_g, norm_ffn_g, sb_w_qkv, sb_g_q, sb_g_k, sb_w_o, pool_w, pool_scale,
           conv_w_in, conv_w, conv_w_out, ffn_w_gate, ffn_w_up, ffn_w_down):
    x = np.asarray(x, np.float32)
    B, S, _ = x.shape
    weights = dict(sb_w_qkv=np.asarray(sb_w_qkv, np.float32), sb_w_o=np.asarray(sb_w_o, np.float32),
                   pool_w=np.asarray(pool_w, np.float32), conv_w_in=np.asarray(conv_w_in, np.float32),
                   conv_w_out=np.asarray(conv_w_out, np.float32), ffn_w_gate=np.asarray(ffn_w_gate, np.float32),
                   ffn_w_up=np.asarray(ffn_w_up, np.float32), ffn_w_down=np.asarray(ffn_w_down, np.float32))
    vecs = make_vecs(np.asarray(norm_mix_g), np.asarray(norm_ffn_g), np.asarray(sb_g_q), np.asarray(sb_g_k),
                     np.asarray(pool_scale), np.asarray(conv_w))
    nc = build(S, LAYERS, NSEQ=1)
    in_maps = make_in_maps(S, LAYERS, [x[b] for b in range(B)], vecs, weights)
    res = run_bass_kernel_spmd(nc, in_maps, core_ids=list(range(B)))
    return np.stack([np.asarray(res.results[b]["out"], np.float32) for b in range(B)], axis=0)
```

```python
import numpy as np
import concourse.bass as bass
import concourse.mybir as mybir
from concourse.bass_utils import run_bass_kernel_spmd

F32 = mybir.dt.float32
BF16 = mybir.dt.bfloat16
AF = mybir.ActivationFunctionType
ALU = mybir.AluOpType

D = 2048
DC = 16
FF = 5632
FC = 44
NH = 16
TT = 512
EPS = 1e-6
DEPTH = 4
POOL_W = (2, 4, 8, 16)
N_CORES = 8

V_NMIX = 0
V_NFFN = 64
V_PSCALE = 128
V_CONVW = 144
V_GQ = 192
V_GK = 194
V_INVCNT = 196
NV = 260


class Prog:
    ENGS = ('pe', 'act', 'dve', 'pool', 'sync')

    def __init__(self, nc):
        self.nc = nc
        self.q = {e: [] for e in self.ENGS}
        self.semnames = []
        self.count = {}
        self.waited = {e: {} for e in self.ENGS}
        self.res = {}

    def _wait(self, eng, sem, val):
        if self.waited[eng].get(sem, 0) >= val:
            return
        self.waited[eng][sem] = val
        self.q[eng].append(('wait', sem, val))

    def op(self, eng, fn, reads=(), writes=(), sem=None, inc=1):
        for k in reads:
            r = self.res.get(k)
            if r and r[0]:
                self._wait(eng, *r[0])
        for k in writes:
            r = self.res.get(k)
            if r:
                if r[0]:
                    self._wait(eng, *r[0])
                for s, v in r[1].items():
                    self._wait(eng, s, v)
        s = sem if sem else 'E_' + eng
        if s not in self.count:
            self.count[s] = 0
            self.semnames.append(s)
        self.count[s] += inc
        tok = (s, self.count[s])
        self.q[eng].append(('op', fn, s, inc))
        for k in reads:
            r = self.res.setdefault(k, [None, {}])
            r[1][s] = max(r[1].get(s, 0), tok[1])
        for k in writes:
            self.res[k] = [tok, {}]
        return tok

    def dma(self, queue, out, in_, sem, reads=(), writes=()):
        return self.op(queue, lambda e: e.dma_start(out=out, in_=in_), reads, writes, sem=sem, inc=16)

    def wait_tok(self, eng, tok):
        self._wait(eng, *tok)

    def barrier(self):
        for s, v in list(self.count.items()):
            if s.startswith('S_c_') or s.startswith('S_rA') or s.startswith('S_rB'):
                continue
            for eng in self.ENGS:
                self._wait(eng, s, v)

    def replay(self):
        nc = self.nc
        sems = {n: nc.alloc_semaphore(n) for n in self.semnames}
        engmap = {'pe': 'tensor', 'act': 'scalar', 'dve': 'vector', 'pool': 'gpsimd', 'sync': 'sync'}
        with nc.Block() as block:
            for eng in self.ENGS:
                items = self.q[eng]

                def body(e, items=items):
                    for it in items:
                        if it[0] == 'wait':
                            e.wait_ge(sems[it[1]], it[2])
                        else:
                            it[1](e).then_inc(sems[it[2]], it[3])

                getattr(block, engmap[eng])(body)


class Ring:
    def __init__(self, P, name, ap, nslots, slot_elems):
        self.P = P
        self.name = name
        self.n = nslots
        self.i = 0
        self.slots = [ap[:, s * slot_elems:(s + 1) * slot_elems] for s in range(nslots)]

    def load(self, src, kc, dep):
        s = self.i % self.n
        self.i += 1
        key = (self.name, s)
        view = self.slots[s][:, 0:kc * 128].rearrange("p (k j) -> p k j", j=128)
        self.P.dma('sync', view, src, 'S_%s%d' % (self.name, s), reads=(dep,), writes=(key,))
        return key, view


def build(T, layers, NSEQ=1, with_ffn=True):
    nc = bass.Bass("TRN2", target_bir_lowering=False)
    NTT = T // TT
    NTB = T // 128
    S = T // NSEQ
    SNT = S // TT
    P = Prog(nc)

    x_in = nc.dram_tensor("x", [T, D], F32, kind="ExternalInput").ap()
    vecs_in = nc.dram_tensor("vecs", [128, NV], F32, kind="ExternalInput").ap()
    out = nc.dram_tensor("out", [T, D], F32, kind="ExternalOutput").ap()
    xres = nc.dram_tensor("xres", [DC, 128, T], F32).ap()
    xres_v = xres.rearrange("c p t -> p c t")

    kinds = [k for k, _ in layers]
    W = {}
    WB = {}

    def decl_w(name, K, M):
        W[name] = nc.dram_tensor(name, [K, M], F32, kind="ExternalInput").ap()
        WB[name] = nc.dram_tensor("b_" + name, [M // 128, 128, K // 128, 128], BF16).ap()

    for kind, li in layers:
        if kind == 'sb':
            decl_w("wqkv%d" % li, D, 3 * D)
            decl_w("wo%d" % li, D, D)
        elif kind == 'pool':
            W["wpool%d" % li] = nc.dram_tensor("wpool%d" % li, [4, 512, 512], F32, kind="ExternalInput").ap()
            WB["wpool%d" % li] = nc.dram_tensor("b_wpool%d" % li, [16, 128, 4, 128], BF16).ap()
        elif kind == 'conv':
            decl_w("wcin%d" % li, D, 3 * D)
            decl_w("wcout%d" % li, D, D)
        if with_ffn:
            decl_w("wg%d" % li, D, FF)
            decl_w("wu%d" % li, D, FF)
            decl_w("wd%d" % li, FF, D)
    if 'sb' in kinds:
        qT_d = nc.dram_tensor("qT_d", [NH, 128, T], BF16).ap()
        kT_d = nc.dram_tensor("kT_d", [NH, 128, T], BF16).ap()
        v_d = nc.dram_tensor("v_d", [NTB, 128, D], BF16).ap()
        oT_d = nc.dram_tensor("oT_d", [NH, 128, T], BF16).ap()

    def sb(name, shape, dt):
        return nc.alloc_sbuf_tensor(name, shape, dt).ap()

    vecs = sb("vecs_sb", [128, NV], F32)
    gqs = sb("gqs", [128, 2], F32)
    ident_f = sb("ident_f", [128, 128], F32)
    ident_b = sb("ident_b", [128, 128], BF16)
    ones_b = sb("ones_b", [128, 128], BF16)
    nones_b = sb("nones_b", [128, 128], BF16)
    ntri_b = sb("ntri_b", [128, 128], BF16)
    mask_b = sb("mask_b", [128, 128], BF16)
    tmpc_f = sb("tmpc_f", [128, 128], F32)
    ringA_t = sb("ringA", [128, 6 * 2048], BF16)
    ringB_t = sb("ringB", [128, 2 * FC * 128], BF16)
    ringA = Ring(P, "rA", ringA_t, 6, 2048)
    ringB = Ring(P, "rB", ringB_t, 2, FC * 128)
    xTb = [sb("xT0", [128, DC, TT], F32), sb("xT1", [128, DC, TT], F32)]
    XS = {'i': 0}
    hT = sb("hT", [128, DC, 16 + TT], BF16)
    big = sb("big", [128, FC * TT], BF16)
    rstd = sb("rstd", [128, 2, TT], F32)
    lnt = sb("lnt", [128, 2, TT], F32)
    sgt = sb("sgt", [128, 2, TT], BF16)
    misc = sb("misc", [128, 10 * 1024], BF16)
    pall = nc.alloc_psum_tensor("pall", [128, 8 * 512], F32).ap()
    ps = [pall[:, i * 512:(i + 1) * 512] for i in range(8)]
    PSK = [('ps', i) for i in range(8)]

    hTd = hT[:, :, 16:16 + TT]
    act = big.rearrange("p (f t) -> p f t", t=TT)

    P.dma('sync', vecs, vecs_in, 'S_vecs', writes=('vecs',))
    P.op('pool', lambda e: e.memset(tmpc_f, 1.0), writes=('tmpc',))
    P.op('pool', lambda e: e.affine_select(out=ident_f, in_=tmpc_f, pattern=[[-1, 128]], compare_op=ALU.is_equal,
                                           fill=0.0, base=0, channel_multiplier=1), reads=('tmpc',), writes=('ident_f',))
    P.op('pool', lambda e: e.tensor_copy(out=ident_b, in_=ident_f), reads=('ident_f',), writes=('ident_b',))
    P.op('pool', lambda e: e.memset(ones_b, 1.0), writes=('ones_b',))
    P.op('pool', lambda e: e.memset(nones_b, -1.0), writes=('nones_b',))
    P.op('pool', lambda e: e.affine_select(out=mask_b, in_=ones_b, pattern=[[1, 128]], compare_op=ALU.is_gt,
                                           fill=0.0, base=0, channel_multiplier=-1), reads=('ones_b',), writes=('mask_b',))
    P.op('pool', lambda e: e.affine_select(out=ntri_b, in_=nones_b, pattern=[[-1, 128]], compare_op=ALU.is_ge,
                                           fill=0.0, base=0, channel_multiplier=1), reads=('nones_b',), writes=('ntri_b',))
    P.op('dve', lambda e: e.tensor_scalar(out=gqs, in0=vecs[:, V_GQ:V_GQ + 2], scalar1=float(128 ** -0.5), scalar2=None,
                                          op0=ALU.mult), reads=('vecs',), writes=('gqs',))
    CONSTS = ('vecs', 'gqs', 'ident_f', 'ident_b', 'ones_b', 'nones_b', 'ntri_b', 'mask_b')

    def cast_w(name):
        w = W[name]
        wb = WB[name]
        key = ('wb', name)
        if name.startswith('wpool'):
            for g in range(4):
                for mo in range(4):
                    src = w[g][:, mo * 128:(mo + 1) * 128].rearrange("(kc p) j -> p kc j", p=128)
                    P.dma('pool', wb[g * 4 + mo], src, 'S_c_' + name)
        else:
            MC = wb.shape[0]
            for m in range(MC):
                src = w[:, m * 128:(m + 1) * 128].rearrange("(kc p) j -> p kc j", p=128)
                P.dma('pool', wb[m], src, 'S_c_' + name)
        P.res[key] = [('S_c_' + name, P.count['S_c_' + name]), {}]
        return key

    cast_keys = {}

    def need_w(name):
        if name not in cast_keys:
            cast_keys[name] = cast_w(name)
        return cast_keys[name]

    def layer_weights(kind, li):
        names = []
        if kind == 'sb':
            names += ["wqkv%d" % li, "wo%d" % li]
        elif kind == 'pool':
            names += ["wpool%d" % li]
        elif kind == 'conv':
            names += ["wcin%d" % li, "wcout%d" % li]
        if with_ffn:
            names += ["wg%d" % li, "wu%d" % li, "wd%d" % li]
        return names

    pending = []

    def tile_hook():
        if pending:
            need_w(pending.pop(0))

    def rmsnorm(gcol):
        sq = act
        xi = XS['i']
        X = xTb[xi]
        for g4 in range(4):
            c0 = g4 * 4
            P.op('act', lambda e, c0=c0: e.activation(out=sq[:, c0:c0 + 4, :], in_=X[:, c0:c0 + 4, :], func=AF.Square),
                 reads=[('xT', xi, c) for c in range(c0, c0 + 4)], writes=[('act', c) for c in range(c0, c0 + 4)])

        def mm(e):
            ins = None
            for c in range(DC):
                ins = e.matmul(ps[6], lhsT=ones_b, rhs=sq[:, c, :], start=(c == 0), stop=(c == DC - 1))
            return ins
        P.op('pe', mm, reads=[('act', c) for c in range(DC)] + ['ones_b'], writes=[PSK[6]])
        P.op('act', lambda e: e.activation(out=lnt[:, 0, :], in_=ps[6], func=AF.Ln, scale=1.0 / D, bias=EPS),
             reads=[PSK[6]], writes=[('lnt', 0)])
        P.op('act', lambda e: e.activation(out=rstd[:, 0, :], in_=lnt[:, 0, :], func=AF.Exp, scale=-0.5),
             reads=[('lnt', 0)], writes=[('rstd', 0)])
        for c in range(DC):
            P.op('dve', lambda e, c=c: e.scalar_tensor_tensor(out=hTd[:, c, :], in0=X[:, c, :],
                                                               scalar=vecs[:, gcol + c:gcol + c + 1], in1=rstd[:, 0, :],
                                                               op0=ALU.mult, op1=ALU.mult),
                 reads=[('xT', xi, c), ('rstd', 0), 'vecs'], writes=[('hT', c)])

    def load_x(tt, bi=None):
        if bi is None:
            bi = XS['i']
        xres_read_deps('sync', tt)
        P.dma('sync', xTb[bi], xres_v[:, :, tt * TT:(tt + 1) * TT], 'S_xld%d' % bi,
              reads=[('xres', tt)], writes=[('xT', bi, c) for c in range(DC)])

    def prefetch_x(tt):
        if tt < NTT:
            load_x(tt, tt % 2)

    def store_x(tt):
        xi = XS['i']
        P.dma('act', xres_v[:, :, tt * TT:(tt + 1) * TT], xTb[xi], 'S_xst%d' % xi,
              reads=[('xT', xi, c) for c in range(DC)], writes=[('xres', tt)])

    def proj_chunk(pbank, wkey, wview, kc_n, rhs_fn, rhs_keys):
        def mm(e):
            ins = None
            for k in range(kc_n):
                ins = e.matmul(ps[pbank], lhsT=wview[:, k, :], rhs=rhs_fn(k), start=(k == 0), stop=(k == kc_n - 1))
            return ins
        return P.op('pe', mm, reads=[wkey] + list(rhs_keys), writes=[PSK[pbank]])

    def ffn(li, nxt=None):
        if not with_ffn:
            if nxt is not None:
                prefetch_x(nxt)
            return
        xi = XS['i']
        X = xTb[xi]
        gcol = V_NFFN + li * 16
        rmsnorm(gcol)
        kg, ku, kd = need_w("wg%d" % li), need_w("wu%d" % li), need_w("wd%d" % li)
        wbg, wbu, wbd = WB["wg%d" % li], WB["wu%d" % li], WB["wd%d" % li]
        hkeys = [('hT', c) for c in range(DC)]
        for f in range(FC):
            b = f % 2
            wk, wv = ringA.load(wbg[f], DC, kg)
            proj_chunk(0 + b, wk, wv, DC, lambda k: hTd[:, k, :], hkeys)
            wk, wv = ringA.load(wbu[f], DC, ku)
            proj_chunk(2 + b, wk, wv, DC, lambda k: hTd[:, k, :], hkeys)
            P.op('act', lambda e, b=b: e.activation(out=sgt[:, b, :], in_=ps[0 + b], func=AF.Silu),
                 reads=[PSK[0 + b]], writes=[('sgt', b)])
            P.op('dve', lambda e, b=b, f=f: e.tensor_tensor(out=act[:, f, :], in0=sgt[:, b, :], in1=ps[2 + b], op=ALU.mult),
                 reads=[('sgt', b), PSK[2 + b]], writes=[('act', f)])
        if nxt is not None:
            prefetch_x(nxt)
        akeys = [('act', f) for f in range(FC)]
        for m in range(DC):
            b = 4 + (m % 2)
            wk, wv = ringB.load(wbd[m], FC, kd)
            proj_chunk(b, wk, wv, FC, lambda k: act[:, k, :], akeys)
            P.op('dve', lambda e, b=b, m=m: e.tensor_tensor(out=X[:, m, :], in0=X[:, m, :], in1=ps[b], op=ALU.add),
                 reads=[PSK[b]], writes=[('xT', xi, m)])

    def transpose_in():
        xin = misc.bitcast(F32).rearrange("p (b n) -> p b n", b=2)[:, :, 0:D]
        stg = big.bitcast(F32)[:, 0:2 * D].rearrange("p (b c t) -> p b c t", b=2, c=DC)
        for tb in range(NTB):
            b = tb % 2
            P.dma('sync', xin[:, b, :], x_in[tb * 128:(tb + 1) * 128, :], 'S_xin%d' % b, writes=[('xin', b)])
            for g in range(4):
                def tr(e, g=g, b=b):
                    ins = None
                    for j in range(4):
                        c = g * 4 + j
                        ins = e.transpose(out=ps[g][:, j * 128:(j + 1) * 128], in_=xin[:, b, c * 128:(c + 1) * 128],
                                          identity=ident_f)
                    return ins
                P.op('pe', tr, reads=[('xin', b), 'ident_f'], writes=[PSK[g]])
                eng = 'act' if g % 2 == 0 else 'dve'
                if eng == 'act':
                    P.op('act', lambda e, g=g, b=b: e.activation(out=stg[:, b, g * 4:(g + 1) * 4, :],
                                                                 in_=ps[g].rearrange("p (c t) -> p c t", c=4), func=AF.Copy),
                         reads=[PSK[g]], writes=[('stg', b, g)])
                else:
                    P.op('dve', lambda e, g=g, b=b: e.tensor_copy(out=stg[:, b, g * 4:(g + 1) * 4, :],
                                                                  in_=ps[g].rearrange("p (c t) -> p c t", c=4)),
                         reads=[PSK[g]], writes=[('stg', b, g)])
            P.dma('act', xres_v[:, :, tb * 128:(tb + 1) * 128], stg[:, b], 'S_stg%d' % b,
                  reads=[('stg', b, g) for g in range(4)], writes=[('xresb', tb)])
        for tt in range(NTT):
            toks = [P.res[('xresb', tt * 4 + j)][0] for j in range(4)]
            P.res[('xres', tt)] = [None, {}]
            for s, v in toks:
                P.res[('xres', tt)][1][s] = max(P.res[('xres', tt)][1].get(s, 0), v)

    def xres_read_deps(eng, tt):
        r = P.res.get(('xres', tt))
        if r:
            for s, v in r[1].items():
                P._wait(eng, s, v)

    def transpose_out():
        xo = misc.bitcast(F32).rearrange("p (b n) -> p b n", b=2)[:, :, 0:D]
        for tt in range(NTT):
            XS['i'] = tt % 2
            xi = XS['i']
            X = xTb[xi]
            load_x(tt)
            for j in range(4):
                tb = tt * 4 + j
                b = tb % 2
                for g in range(4):
                    def tr(e, g=g, j=j, X=X):
                        ins = None
                        for i in range(4):
                            c = g * 4 + i
                            ins = e.transpose(out=ps[g][:, i * 128:(i + 1) * 128], in_=X[:, c, j * 128:(j + 1) * 128],
                                              identity=ident_f)
                        return ins
                    P.op('pe', tr, reads=[('xT', xi, c) for c in range(g * 4, g * 4 + 4)] + ['ident_f'], writes=[PSK[g]])
                    if g % 2 == 0:
                        P.op('act', lambda e, g=g, b=b: e.activation(out=xo[:, b, g * 512:(g + 1) * 512], in_=ps[g], func=AF.Copy),
                             reads=[PSK[g]], writes=[('xo', b, g)])
                    else:
                        P.op('dve', lambda e, g=g, b=b: e.tensor_copy(out=xo[:, b, g * 512:(g + 1) * 512], in_=ps[g]),
                             reads=[PSK[g]], writes=[('xo', b, g)])
                P.dma('act', out[tb * 128:(tb + 1) * 128, :], xo[:, b, :], 'S_out%d' % b,
                      reads=[('xo', b, g) for g in range(4)], writes=[('outb', b)])
        for b in range(2):
            r = P.res.get(('outb', b))
            if r and r[0]:
                P.wait_tok('act', r[0])
                P.wait_tok('pool', r[0])
                P.wait_tok('sync', r[0])

    def sb_phase_a(li, j):
        kq = need_w("wqkv%d" % li)
        wb = WB["wqkv%d" % li]
        hkeys = [('hT', c) for c in range(DC)]
        qst = misc[:, 0:2 * TT].rearrange("p (b t) -> p b t", b=2)
        sqh = misc[:, 2 * TT:4 * TT].rearrange("p (b t) -> p b t", b=2)
        vst = misc[:, 4 * TT:4 * TT + 4 * D].rearrange("p (tb n) -> p tb n", tb=4)
        vT = sgt
        for tt in range(NTT):
            tile_hook()
            XS['i'] = tt % 2
            xi = XS['i']
            X = xTb[xi]
            if tt == 0:
                load_x(0, 0)
            rmsnorm(V_NMIX + li * 16)
            prefetch_x(tt + 1)
            it = 0
            deferred = []
            tails = []

            def qk_tail(which, hd, pb, b, dst):
                P.op('act', lambda e: e.activation(out=sqh[:, b, :], in_=ps[pb], func=AF.Square),
                     reads=[PSK[pb]], writes=[('sqh', b)])
                P.op('pe', lambda e: e.matmul(ps[4 + b], lhsT=ones_b, rhs=sqh[:, b, :], start=True, stop=True),
                     reads=[('sqh', b), 'ones_b'], writes=[PSK[4 + b]])
                P.op('act', lambda e: e.activation(out=lnt[:, b, :], in_=ps[4 + b], func=AF.Ln, scale=1.0 / 128, bias=EPS),
                     reads=[PSK[4 + b]], writes=[('lnt', b)])
                P.op('act', lambda e: e.activation(out=rstd[:, b, :], in_=lnt[:, b, :], func=AF.Exp, scale=-0.5),
                     reads=[('lnt', b)], writes=[('rstd', b)])
                gsc = gqs[:, j:j + 1] if which == 0 else vecs[:, V_GK + j:V_GK + j + 1]
                P.op('dve', lambda e: e.scalar_tensor_tensor(out=qst[:, b, :], in0=ps[pb], scalar=gsc,
                                                             in1=rstd[:, b, :], op0=ALU.mult, op1=ALU.mult),
                     reads=[PSK[pb], ('rstd', b), 'gqs', 'vecs'], writes=[('qst', b)])
                if deferred:
                    deferred.pop()()
                deferred.append(lambda: P.dma('act', dst[hd][:, tt * TT:(tt + 1) * TT], qst[:, b, :], 'S_qst%d' % b,
                                              reads=[('qst', b)], writes=[('qkd', which, hd, tt)]))

            def v_tail(hd, pb, b):
                pvb = ps[6 + b].bitcast(BF16)
                P.op('act', lambda e: e.activation(out=vT[:, b, :], in_=ps[pb], func=AF.Copy),
                     reads=[PSK[pb]], writes=[('vT', b)])

                def tr(e):
                    ins = None
                    for tb in range(4):
                        ins = e.transpose(out=pvb[:, tb * 128:(tb + 1) * 128], in_=vT[:, b, tb * 128:(tb + 1) * 128],
                                          identity=ident_b)
                    return ins
                P.op('pe', tr, reads=[('vT', b), 'ident_b'], writes=[PSK[6 + b]])
                P.op('dve', lambda e: e.tensor_copy(out=vst[:, :, hd * 128:(hd + 1) * 128],
                                                    in_=pvb[:, 0:512].rearrange("p (tb d) -> p tb d", tb=4)),
                     reads=[PSK[6 + b]], writes=[('vst', hd)])

            for which in range(3):
                dst = qT_d if which == 0 else kT_d
                for hd in range(NH):
                    pb = it % 4
                    b = it % 2
                    it += 1
                    wk, wv = ringA.load(wb[which * NH + hd], DC, kq)
                    proj_chunk(pb, wk, wv, DC, lambda k: hTd[:, k, :], hkeys)
                    if tails:
                        tails.pop()()
                    if which < 2:
                        tails.append(lambda which=which, hd=hd, pb=pb, b=b, dst=dst: qk_tail(which, hd, pb, b, dst))
                    else:
                        tails.append(lambda hd=hd, pb=pb, b=b: v_tail(hd, pb, b))
            tails.pop()()
            if deferred:
                deferred.pop()()
            P.dma('act', v_d[tt * 4:(tt + 1) * 4].rearrange("tb p n -> p tb n"), vst, 'S_vst',
                  reads=[('vst', hd) for hd in range(NH)], writes=[('vd', tt)])

    def sb_phase_b(li):
        HB = 3 * S
        assert HB <= 2 * DC * TT and HB <= FC * TT
        hb = [big[:, 0:HB], xTb[1].rearrange("p c t -> p (c t)").bitcast(BF16)[:, 0:HB]]
        sp2 = misc[:, 0:3072].rearrange("p (b t) -> p b t", b=3)
        at2 = misc[:, 3072:6144].rearrange("p (b t) -> p b t", b=3)
        NSL = 6
        srun = misc[:, 6144:6144 + NSL * TT].rearrange("p (b t) -> p b t", b=NSL)
        ost = misc[:, 6144 + NSL * TT:6144 + (NSL + 2) * TT].rearrange("p (b t) -> p b t", b=2)
        et2 = lnt.rearrange("p b t -> p (b t)")
        pZ = pall[:, 0:1024]
        pR = pall[:, 1024:2048]
        units = []
        for hd in range(NSEQ * NH):
            for qi in range(SNT):
                for kb in range(4 * qi + 3, 4 * qi - 1, -1):
                    units.append((hd, qi, [kb]))
                for kb in range(4 * qi - 1, 0, -2):
                    units.append((hd, qi, [kb, kb - 1]))
        NU = len(units)
        loaded = set()
        head_start = {}
        for ii, u in enumerate(units):
            head_start.setdefault(u[0], ii)
        info = [dict() for _ in range(NU)]
        state = {'slot': 0}

        def load_head(hd):
            if hd in loaded or hd >= NSEQ * NH:
                return
            loaded.add(hd)
            sq_, h_ = hd // NH, hd % NH
            t0_ = sq_ * S
            hbuf = hb[hd % 2]
            key = ('hb', hd % 2)
            sem = 'S_hb%d' % (hd % 2)
            P.dma('sync', hbuf[:, 0:S], qT_d[h_][:, t0_:t0_ + S], sem, writes=[key])
            P.op('sync', lambda e: e.dma_start(out=hbuf[:, S:2 * S], in_=kT_d[h_][:, t0_:t0_ + S]), sem=sem, inc=16)
            tok = P.op('sync', lambda e: e.dma_start(
                out=hbuf[:, 2 * S:3 * S].rearrange("p (tb d) -> p tb d", d=128),
                in_=v_d[t0_ // 128:(t0_ + S) // 128, :, h_ * 128:(h_ + 1) * 128].rearrange("tb p d -> p tb d")),
                sem=sem, inc=16)
            P.res[key] = [tok, {}]

        def geom(i):
            hd, qi, kbs = units[i]
            m = kbs[0] - 4 * qi
            c0 = 128 * m if m > 0 else 0
            return hd, qi, kbs, m, c0, TT - c0

        def stage1(i):
            hd, qi, kbs, m, c0, N = geom(i)
            if i == 0:
                load_head(0)
            if i == head_start[hd] + 3:
                load_head(hd + 1)
            hbuf = hb[hd % 2]
            hk = ('hb', hd % 2)
            W = N if len(kbs) == 1 else 2 * TT
            qs = hbuf[:, qi * TT + c0:(qi + 1) * TT]

            def mm(e):
                ins = None
                for si, kb in enumerate(kbs):
                    ins = e.matmul(pZ[:, si * TT:si * TT + N], lhsT=hbuf[:, S + kb * 128:S + (kb + 1) * 128], rhs=qs,
                                   start=True, stop=True)
                return ins
            P.op('pe', mm, reads=[hk], writes=['pz'])
            P.op('act', lambda e: e.activation(out=et2[:, 0:W], in_=pZ[:, 0:W], func=AF.Exp),
                 reads=['pz'], writes=['et'])
            sp = sp2[:, i % 3, :]
            P.op('act', lambda e: e.activation(out=sp[:, 0:W], in_=et2[:, 0:W], func=AF.Ln, bias=1.0),
                 reads=['et'], writes=[('sp', i % 3)])
            if m >= 0:
                P.op('dve', lambda e: e.tensor_tensor(out=sp[:, 0:128], in0=sp[:, 0:128], in1=mask_b, op=ALU.mult),
                     reads=['mask_b'], writes=[('sp', i % 3)])
            first = (kbs[0] == 4 * qi + 3)
            prev = []
            for si, kb in enumerate(kbs):
                old = state['slot']
                new = (old + 1) % NSL
                state['slot'] = new
                sn = srun[:, new, :]
                so = srun[:, old, :]
                if c0 > 0:
                    P.op('dve', lambda e, sn=sn: e.memset(sn[:, 0:c0], 0.0), writes=[('srun', new)])
                if first and si == 0:
                    prev.append(None)
                    P.op('dve', lambda e, sn=sn: e.tensor_copy(out=sn[:, c0:TT], in_=sp[:, 0:N]),
                         reads=[('sp', i % 3)], writes=[('srun', new)])
                else:
                    prev.append(old)
                    P.op('dve', lambda e, sn=sn, so=so, si=si: e.tensor_tensor(out=sn[:, c0:TT], in0=so[:, c0:TT],
                                                                              in1=sp[:, si * TT:si * TT + N], op=ALU.add),
                         reads=[('sp', i % 3), ('srun', old)], writes=[('srun', new)])
            info[i]['prev'] = prev
            info[i]['first'] = first

        def stage2(i):
            hd, qi, kbs, m, c0, N = geom(i)
            hbuf = hb[hd % 2]
            hk = ('hb', hd % 2)
            W = N if len(kbs) == 1 else 2 * TT
            qs = hbuf[:, qi * TT + c0:(qi + 1) * TT]
            sp = sp2[:, i % 3, :]
            prev = info[i]['prev']

            def mm(e):
                ins = None
                for si, kb in enumerate(kbs):
                    o = pR[:, si * TT:si * TT + N]
                    e.matmul(o, lhsT=ntri_b, rhs=sp[:, si * TT:si * TT + N], start=True, stop=False)
                    if prev[si] is not None:
                        e.matmul(o, lhsT=nones_b, rhs=srun[:, prev[si], c0:TT], start=False, stop=False)
                    ins = e.matmul(o, lhsT=hbuf[:, S + kb * 128:S + (kb + 1) * 128], rhs=qs, start=False, stop=True)
                return ins
            rd = [hk, ('sp', i % 3), 'ntri_b', 'nones_b'] + [('srun', p) for p in prev if p is not None]
            P.op('pe', mm, reads=rd, writes=['pr'])
            at = at2[:, i % 3, :]
            P.op('act', lambda e: e.activation(out=at[:, 0:W], in_=pR[:, 0:W], func=AF.Exp),
                 reads=['pr'], writes=[('at', i % 3)])
            if m >= 0:
                P.op('dve', lambda e: e.tensor_tensor(out=at[:, 0:128], in0=at[:, 0:128], in1=mask_b, op=ALU.mult),
                     reads=['mask_b'], writes=[('at', i % 3)])

        def stage3(i):
            hd, qi, kbs, m, c0, N = geom(i)
            hbuf = hb[hd % 2]
            hk = ('hb', hd % 2)
            ob = 4 + (hd * SNT + qi) % 2
            first = info[i]['first']
            at = at2[:, i % 3, :]

            def mm(e):
                ins = None
                for si, kb in enumerate(kbs):
                    ins = e.matmul(ps[ob][:, c0:TT], lhsT=hbuf[:, 2 * S + kb * 128:2 * S + (kb + 1) * 128],
                                   rhs=at[:, si * TT:si * TT + N], start=(first and si == 0), stop=(kb == 0),
                                   skip_group_check=True)
                return ins
            P.op('pe', mm, reads=[hk, ('at', i % 3)], writes=[PSK[ob]])
            if kbs[-1] == 0:
                sbuf = (hd * SNT + qi) % 2
                gt0 = (hd // NH) * S + qi * TT
                P.op('dve', lambda e: e.tensor_copy(out=ost[:, sbuf, :], in_=ps[ob]),
                     reads=[PSK[ob]], writes=[('ost', sbuf)])
                P.dma('act', oT_d[hd % NH][:, gt0:gt0 + TT], ost[:, sbuf, :], 'S_ost%d' % sbuf,
                      reads=[('ost', sbuf)], writes=[('od', hd, qi)])

        for n in range(NU + 2):
            if n < NU:
                stage1(n)
            if 0 <= n - 1 < NU:
                stage2(n - 1)
            if 0 <= n - 2 < NU:
                stage3(n - 2)

    def sb_phase_c(li):
        ko = need_w("wo%d" % li)
        wb = WB["wo%d" % li]
        oT = act
        for tt in range(NTT):
            tile_hook()
            XS['i'] = tt % 2
            xi = XS['i']
            X = xTb[xi]
            if tt == 0:
                load_x(0, 0)
            P.dma('sync', oT[:, 0:NH, :], oT_d[:, :, tt * TT:(tt + 1) * TT].rearrange("h p t -> p h t"), 'S_old',
                  writes=[('act', h) for h in range(NH)])
            okeys = [('act', h) for h in range(NH)]
            for m in range(DC):
                b = m % 2
                wk, wv = ringA.load(wb[m], DC, ko)
                proj_chunk(b, wk, wv, DC, lambda k: oT[:, k, :], okeys)
                P.op('dve', lambda e, b=b, m=m, X=X: e.tensor_tensor(out=X[:, m, :], in0=X[:, m, :], in1=ps[b], op=ALU.add),
                     reads=[PSK[b]], writes=[('xT', xi, m)])
            ffn(li, tt + 1)
            store_x(tt)

    def pool_layer(li):
        kp = need_w("wpool%d" % li)
        wb = WB["wpool%d" % li]
        pT = act
        U = 16 + TT
        wa = misc.bitcast(F32)[:, 0:4 * U].rearrange("p (c u) -> p c u", c=4)
        wb2 = misc.bitcast(F32)[:, 4 * U:8 * U].rearrange("p (c u) -> p c u", c=4)
        for tt in range(NTT):
            tile_hook()
            XS['i'] = tt % 2
            xi = XS['i']
            X = xTb[xi]
            if tt == 0:
                load_x(0, 0)
            if tt % SNT == 0:
                P.op('dve', lambda e: e.memset(hT[:, :, 0:16], 0.0), writes=[('hThalo',)])
            rmsnorm(V_NMIX + li * 16)
            for g in range(4):
                w = POOL_W[g]
                cs = slice(g * 4, g * 4 + 4)
                hk = [('hT', c) for c in range(g * 4, g * 4 + 4)] + [('hThalo',)]
                P.op('dve', lambda e, cs=cs: e.tensor_tensor(out=wa[:, :, 1:U], in0=hT[:, cs, 1:U], in1=hT[:, cs, 0:U - 1], op=ALU.add),
                     reads=hk, writes=['wa'])
                cur, oth, curk, othk = wa, wb2, 'wa', 'wb'
                sh = 2
                lo = 1
                while sh < w:
                    lo2 = lo + sh
                    P.op('dve', lambda e, cur=cur, oth=oth, lo2=lo2, sh=sh: e.tensor_tensor(
                        out=oth[:, :, lo2:U], in0=cur[:, :, lo2:U], in1=cur[:, :, lo2 - sh:U - sh], op=ALU.add),
                        reads=[curk], writes=[othk])
                    cur, oth, curk, othk = oth, cur, othk, curk
                    lo = lo2
                    sh *= 2
                P.op('dve', lambda e, cur=cur, cs=cs, w=w: e.scalar_tensor_tensor(
                    out=pT[:, cs, :], in0=cur[:, :, 16:U], scalar=1.0 / w, in1=hT[:, cs, 16:U], op0=ALU.mult, op1=ALU.subtract),
                    reads=[curk] + hk, writes=[('act', c) for c in range(g * 4, g * 4 + 4)])
                if tt % SNT == 0:
                    nfix = w - 1
                    for c in range(4):
                        P.op('dve', lambda e, cur=cur, c=c, g=g, nfix=nfix: e.tensor_tensor(
                            out=cur[:, c, 16:16 + nfix], in0=cur[:, c, 16:16 + nfix],
                            in1=vecs[:, V_INVCNT + g * 16:V_INVCNT + g * 16 + nfix], op=ALU.mult),
                            reads=[curk, 'vecs'], writes=[curk])
                        P.op('dve', lambda e, cur=cur, c=c, g=g, nfix=nfix: e.tensor_tensor(
                            out=pT[:, g * 4 + c, 0:nfix], in0=cur[:, c, 16:16 + nfix], in1=hT[:, g * 4 + c, 16:16 + nfix],
                            op=ALU.subtract),
                            reads=[curk] + hk, writes=[('act', g * 4 + c)])
                for mo in range(4):
                    c = g * 4 + mo
                    b = c % 2
                    wk, wv = ringA.load(wb[c], 4, kp)
                    proj_chunk(b, wk, wv, 4, lambda k, g=g: pT[:, g * 4 + k, :], [('act', g * 4 + k) for k in range(4)])
                    P.op('dve', lambda e, b=b, c=c, X=X: e.scalar_tensor_tensor(
                        out=X[:, c, :], in0=ps[b], scalar=vecs[:, V_PSCALE + c:V_PSCALE + c + 1], in1=X[:, c, :],
                        op0=ALU.mult, op1=ALU.add),
                        reads=[PSK[b], 'vecs'], writes=[('xT', xi, c)])
            if (tt + 1) % SNT != 0:
                P.op('act', lambda e: e.activation(out=hT[:, :, 0:16], in_=hT[:, :, TT:TT + 16], func=AF.Copy),
                     reads=[('hT', c) for c in range(DC)], writes=[('hThalo',)])
            ffn(li, tt + 1)
            if tt + 1 < NTT:
                pass
            store_x(tt)

    def conv_layer(li):
        kci, kco = need_w("wcin%d" % li), need_w("wcout%d" % li)
        wbi, wbo = WB["wcin%d" % li], WB["wcout%d" % li]
        byT = act
        mf = misc.bitcast(F32)
        gb = mf[:, 0:2 * 516].rearrange("p (b u) -> p b u", b=2)
        ub = mf[:, 1032:1032 + 2 * TT].rearrange("p (b t) -> p b t", b=2)
        yb = mf[:, 2056:2056 + 2 * TT].rearrange("p (b t) -> p b t", b=2)
        gh = mf[:, 3080:3080 + 2 * DC].rearrange("p (c u) -> p c u", u=2)
        hkeys = [('hT', c) for c in range(DC)]
        for tt in range(NTT):
            tile_hook()
            XS['i'] = tt % 2
            xi = XS['i']
            X = xTb[xi]
            if tt == 0:
                load_x(0, 0)
            rmsnorm(V_NMIX + li * 16)
            if tt % SNT == 0:
                P.op('dve', lambda e: e.memset(gh, 0.0), writes=[('gh', c) for c in range(DC)])
            for m in range(DC):
                b = m % 2
                wk, wv = ringA.load(wbi[m], DC, kci)
                proj_chunk(0 + b, wk, wv, DC, lambda k: hTd[:, k, :], hkeys)
                wk, wv = ringA.load(wbi[DC + m], DC, kci)
                proj_chunk(2 + b, wk, wv, DC, lambda k: hTd[:, k, :], hkeys)
                wk, wv = ringA.load(wbi[2 * DC + m], DC, kci)
                proj_chunk(4 + b, wk, wv, DC, lambda k: hTd[:, k, :], hkeys)
                P.op('act', lambda e, b=b: e.activation(out=ub[:, b, :], in_=ps[4 + b], func=AF.Copy),
                     reads=[PSK[4 + b]], writes=[('ub', b)])
                P.op('dve', lambda e, b=b, m=m: e.tensor_copy(out=gb[:, b, 0:2], in_=gh[:, m, :]),
                     reads=[('gh', m)], writes=[('gb', b)])
                P.op('dve', lambda e, b=b: e.tensor_tensor(out=gb[:, b, 2:2 + TT], in0=ps[2 + b], in1=ub[:, b, :], op=ALU.mult),
                     reads=[PSK[2 + b], ('ub', b)], writes=[('gb', b)])
                P.op('dve', lambda e, b=b, m=m: e.tensor_copy(out=gh[:, m, :], in_=gb[:, b, TT:TT + 2]),
                     reads=[('gb', b)], writes=[('gh', m)])
                cw = lambda jj, m=m: vecs[:, V_CONVW + jj * 16 + m:V_CONVW + jj * 16 + m + 1]
                P.op('dve', lambda e, b=b, cw=cw: e.tensor_scalar(out=yb[:, b, :], in0=gb[:, b, 2:2 + TT], scalar1=cw(2), scalar2=None,
                                                                  op0=ALU.mult),
                     reads=[('gb', b), 'vecs'], writes=[('yb', b)])
                P.op('dve', lambda e, b=b, cw=cw: e.scalar_tensor_tensor(out=yb[:, b, :], in0=gb[:, b, 1:1 + TT], scalar=cw(1),
                                                                         in1=yb[:, b, :], op0=ALU.mult, op1=ALU.add),
                     reads=[('gb', b), 'vecs'], writes=[('yb', b)])
                P.op('dve', lambda e, b=b, cw=cw: e.scalar_tensor_tensor(out=yb[:, b, :], in0=gb[:, b, 0:TT], scalar=cw(0),
                                                                         in1=yb[:, b, :], op0=ALU.mult, op1=ALU.add),
                     reads=[('gb', b), 'vecs'], writes=[('yb', b)])
                P.op('dve', lambda e, b=b, m=m: e.tensor_tensor(out=byT[:, m, :], in0=ps[0 + b], in1=yb[:, b, :], op=ALU.mult),
                     reads=[PSK[0 + b], ('yb', b)], writes=[('act', m)])
            bkeys = [('act', m) for m in range(DC)]
            for mo in range(DC):
                b = 6 + mo % 2
                wk, wv = ringA.load(wbo[mo], DC, kco)
                proj_chunk(b, wk, wv, DC, lambda k: byT[:, k, :], bkeys)
                P.op('dve', lambda e, b=b, mo=mo, X=X: e.tensor_tensor(out=X[:, mo, :], in0=X[:, mo, :], in1=ps[b], op=ALU.add),
                     reads=[PSK[b]], writes=[('xT', xi, mo)])
            ffn(li, tt + 1)
            store_x(tt)

    def ffn_only_layer(li):
        for tt in range(NTT):
            tile_hook()
            XS['i'] = tt % 2
            xi = XS['i']
            X = xTb[xi]
            if tt == 0:
                load_x(0, 0)
            ffn(li, tt + 1)
            store_x(tt)

    first = layer_weights(*layers[0])
    need_w(first[0])
    transpose_in()
    for nm in first[1:]:
        need_w(nm)
    P.barrier()
    for idx, (kind, li) in enumerate(layers):
        j = li // 3
        if idx + 1 < len(layers):
            pending.extend(layer_weights(*layers[idx + 1]))
        if kind == 'sb':
            sb_phase_a(li, j)
            P.barrier()
            sb_phase_b(li)
            P.barrier()
            sb_phase_c(li)
        elif kind == 'pool':
            pool_layer(li)
        elif kind == 'conv':
            conv_layer(li)
        elif kind == 'ffn':
            ffn_only_layer(li)
        while pending:
            need_w(pending.pop(0))
        P.barrier()
    transpose_out()
    P.replay()
    return nc


LAYERS = [('sb', 0), ('pool', 1), ('conv', 2), ('sb', 3)]


def make_vecs(norm_mix_g, norm_ffn_g, sb_g_q, sb_g_k, pool_scale, conv_w):
    v = np.zeros((128, NV), np.float32)
    fm = lambda a: np.asarray(a, np.float32).reshape(DC, 128).T
    for l in range(DEPTH):
        v[:, V_NMIX + l * 16:V_NMIX + (l + 1) * 16] = fm(norm_mix_g[l])
        v[:, V_NFFN + l * 16:V_NFFN + (l + 1) * 16] = fm(norm_ffn_g[l])
    v[:, V_PSCALE:V_PSCALE + 16] = fm(pool_scale[0])
    for jj in range(3):
        v[:, V_CONVW + jj * 16:V_CONVW + (jj + 1) * 16] = fm(conv_w[0][jj])
    for j in range(2):
        v[:, V_GQ + j] = np.asarray(sb_g_q[j], np.float32)
        v[:, V_GK + j] = np.asarray(sb_g_k[j], np.float32)
    for g, w in enumerate(POOL_W):
        v[:, V_INVCNT + g * 16:V_INVCNT + (g + 1) * 16] = (1.0 / np.minimum(np.arange(1, 17), w)).astype(np.float32)[None, :]
    return v


def make_in_maps(T, layers, x_shards, vecs, weights, with_ffn=True):
    maps = []
    for xs in x_shards:
        m = {"x": np.ascontiguousarray(xs, dtype=np.float32), "vecs": vecs}
        for kind, li in layers:
            j = li // 3
            if kind == 'sb':
                m["wqkv%d" % li] = weights["sb_w_qkv"][j]
                m["wo%d" % li] = weights["sb_w_o"][j]
            elif kind == 'pool':
                m["wpool%d" % li] = weights["pool_w"][0]
            elif kind == 'conv':
                m["wcin%d" % li] = weights["conv_w_in"][0]
                m["wcout%d" % li] = weights["conv_w_out"][0]
            if with_ffn:
                m["wg%d" % li] = weights["ffn_w_gate"][li]
                m["wu%d" % li] = weights["ffn_w_up"][li]
                m["wd%d" % li] = weights["ffn_w_down"][li]
        maps.append(m)
    return maps


def kernel(x, norm_mix_g, norm_ffn_g, sb_w_qkv, sb_g_q, sb_g_k, sb_w_o, pool_w, pool_scale,
           conv_w_in, conv_w, conv_w_out, ffn_w_gate, ffn_w_up, ffn_w_down):
    x = np.asarray(x, np.float32)
    B, S, _ = x.shape
    weights = dict(sb_w_qkv=np.asarray(sb_w_qkv, np.float32), sb_w_o=np.asarray(sb_w_o, np.float32),
                   pool_w=np.asarray(pool_w, np.float32), conv_w_in=np.asarray(conv_w_in, np.float32),
                   conv_w_out=np.asarray(conv_w_out, np.float32), ffn_w_gate=np.asarray(ffn_w_gate, np.float32),
                   ffn_w_up=np.asarray(ffn_w_up, np.float32), ffn_w_down=np.asarray(ffn_w_down, np.float32))
    vecs = make_vecs(np.asarray(norm_mix_g), np.asarray(norm_ffn_g), np.asarray(sb_g_q), np.asarray(sb_g_k),
                     np.asarray(pool_scale), np.asarray(conv_w))
    nc = build(S, LAYERS, NSEQ=1)
    work = [0, 1, 4, 5][:B]
    real = make_in_maps(S, LAYERS, [x[b] for b in range(B)], vecs, weights)
    zero = {k: np.zeros_like(v) for k, v in real[0].items()}
    in_maps = [zero] * N_CORES
    in_maps = list(in_maps)
    for b, c in enumerate(work):
        in_maps[c] = real[b]
    res = run_bass_kernel_spmd(nc, in_maps, core_ids=list(range(N_CORES)))
    return np.stack([np.asarray(res.results[c]["out"], np.float32) for c in work], axis=0)
```

```python
import numpy as np
import concourse.bass as bass
import concourse.mybir as mybir
from concourse.bass_utils import run_bass_kernel_spmd

F32 = mybir.dt.float32
BF16 = mybir.dt.bfloat16
AF = mybir.ActivationFunctionType
ALU = mybir.AluOpType

D = 2048
DC = 16
FF = 5632
FC = 44
NH = 16
TT = 512
EPS = 1e-6
DEPTH = 4
POOL_W = (2, 4, 8, 16)
N_CORES = 8

V_NMIX = 0
V_NFFN = 64
V_PSCALE = 128
V_CONVW = 144
V_GQ = 192
V_GK = 194
V_INVCNT = 196
NV = 260


class Prog:
    ENGS = ('pe', 'act', 'dve', 'pool', 'sync')

    def __init__(self, nc):
        self.nc = nc
        self.q = {e: [] for e in self.ENGS}
        self.semnames = []
        self.count = {}
        self.waited = {e: {} for e in self.ENGS}
        self.res = {}

    def _wait(self, eng, sem, val):
        if self.waited[eng].get(sem, 0) >= val:
            return
        self.waited[eng][sem] = val
        self.q[eng].append(('wait', sem, val))

    def op(self, eng, fn, reads=(), writes=(), sem=None, inc=1):
        for k in reads:
            r = self.res.get(k)
            if r and r[0]:
                self._wait(eng, *r[0])
        for k in writes:
            r = self.res.get(k)
            if r:
                if r[0]:
                    self._wait(eng, *r[0])
                for s, v in r[1].items():
                    self._wait(eng, s, v)
        s = sem if sem else 'E_' + eng
        if s not in self.count:
            self.count[s] = 0
            self.semnames.append(s)
        self.count[s] += inc
        tok = (s, self.count[s])
        self.q[eng].append(('op', fn, s, inc))
        for k in reads:
            r = self.res.setdefault(k, [None, {}])
            r[1][s] = max(r[1].get(s, 0), tok[1])
        for k in writes:
            self.res[k] = [tok, {}]
        return tok

    def dma(self, queue, out, in_, sem, reads=(), writes=()):
        return self.op(queue, lambda e: e.dma_start(out=out, in_=in_), reads, writes, sem=sem, inc=16)

    def wait_tok(self, eng, tok):
        self._wait(eng, *tok)

    def barrier(self):
        for s, v in list(self.count.items()):
            if s.startswith('S_c_') or s.startswith('S_rA') or s.startswith('S_rB'):
                continue
            for eng in self.ENGS:
                self._wait(eng, s, v)

    def replay(self):
        nc = self.nc
        sems = {n: nc.alloc_semaphore(n) for n in self.semnames}
        engmap = {'pe': 'tensor', 'act': 'scalar', 'dve': 'vector', 'pool': 'gpsimd', 'sync': 'sync'}
        with nc.Block() as block:
            for eng in self.ENGS:
                items = self.q[eng]

                def body(e, items=items):
                    for it in items:
                        if it[0] == 'wait':
                            e.wait_ge(sems[it[1]], it[2])
                        else:
                            it[1](e).then_inc(sems[it[2]], it[3])

                getattr(block, engmap[eng])(body)


class Ring:
    def __init__(self, P, name, ap, nslots, slot_elems):
        self.P = P
        self.name = name
        self.n = nslots
        self.i = 0
        self.slots = [ap[:, s * slot_elems:(s + 1) * slot_elems] for s in range(nslots)]

    def load(self, src, kc, dep):
        s = self.i % self.n
        self.i += 1
        key = (self.name, s)
        view = self.slots[s][:, 0:kc * 128].rearrange("p (k j) -> p k j", j=128)
        self.P.dma('sync', view, src, 'S_%s%d' % (self.name, s), reads=(dep,), writes=(key,))
        return key, view


def build(T, layers, NSEQ=1, with_ffn=True):
    nc = bass.Bass("TRN2", target_bir_lowering=False)
    NTT = T // TT
    NTB = T // 128
    S = T // NSEQ
    SNT = S // TT
    P = Prog(nc)

    x_in = nc.dram_tensor("x", [T, D], F32, kind="ExternalInput").ap()
    vecs_in = nc.dram_tensor("vecs", [128, NV], F32, kind="ExternalInput").ap()
    out = nc.dram_tensor("out", [T, D], F32, kind="ExternalOutput").ap()
    xres = nc.dram_tensor("xres", [DC, 128, T], F32).ap()
    xres_v = xres.rearrange("c p t -> p c t")

    kinds = [k for k, _ in layers]
    W = {}
    WB = {}

    def decl_w(name, K, M):
        W[name] = nc.dram_tensor(name, [K, M], F32, kind="ExternalInput").ap()
        WB[name] = nc.dram_tensor("b_" + name, [M // 128, 128, K // 128, 128], BF16).ap()

    for kind, li in layers:
        if kind == 'sb':
            decl_w("wqkv%d" % li, D, 3 * D)
            decl_w("wo%d" % li, D, D)
        elif kind == 'pool':
            W["wpool%d" % li] = nc.dram_tensor("wpool%d" % li, [4, 512, 512], F32, kind="ExternalInput").ap()
            WB["wpool%d" % li] = nc.dram_tensor("b_wpool%d" % li, [16, 128, 4, 128], BF16).ap()
        elif kind == 'conv':
            decl_w("wcin%d" % li, D, 3 * D)
            decl_w("wcout%d" % li, D, D)
        if with_ffn:
            decl_w("wg%d" % li, D, FF)
            decl_w("wu%d" % li, D, FF)
            decl_w("wd%d" % li, FF, D)
    if 'sb' in kinds:
        qT_d = nc.dram_tensor("qT_d", [NH, 128, T], BF16).ap()
        kT_d = nc.dram_tensor("kT_d", [NH, 128, T], BF16).ap()
        v_d = nc.dram_tensor("v_d", [NTB, 128, D], BF16).ap()
        oT_d = nc.dram_tensor("oT_d", [NH, 128, T], BF16).ap()

    def sb(name, shape, dt):
        return nc.alloc_sbuf_tensor(name, shape, dt).ap()

    vecs = sb("vecs_sb", [128, NV], F32)
    gqs = sb("gqs", [128, 2], F32)
    ident_f = sb("ident_f", [128, 128], F32)
    ident_b = sb("ident_b", [128, 128], BF16)
    ones_b = sb("ones_b", [128, 128], BF16)
    nones_b = sb("nones_b", [128, 128], BF16)
    ntri_b = sb("ntri_b", [128, 128], BF16)
    mask_b = sb("mask_b", [128, 128], BF16)
    tmpc_f = sb("tmpc_f", [128, 128], F32)
    ringA_t = sb("ringA", [128, 6 * 2048], BF16)
    ringB_t = sb("ringB", [128, 2 * FC * 128], BF16)
    ringA = Ring(P, "rA", ringA_t, 6, 2048)
    ringB = Ring(P, "rB", ringB_t, 2, FC * 128)
    xTb = [sb("xT0", [128, DC, TT], F32), sb("xT1", [128, DC, TT], F32)]
    XS = {'i': 0}
    hT = sb("hT", [128, DC, 16 + TT], BF16)
    big = sb("big", [128, FC * TT], BF16)
    rstd = sb("rstd", [128, 2, TT], F32)
    lnt = sb("lnt", [128, 2, TT], F32)
    sgt = sb("sgt", [128, 2, TT], BF16)
    misc = sb("misc", [128, 10 * 1024], BF16)
    pall = nc.alloc_psum_tensor("pall", [128, 8 * 512], F32).ap()
    ps = [pall[:, i * 512:(i + 1) * 512] for i in range(8)]
    PSK = [('ps', i) for i in range(8)]

    hTd = hT[:, :, 16:16 + TT]
    act = big.rearrange("p (f t) -> p f t", t=TT)

    P.dma('sync', vecs, vecs_in, 'S_vecs', writes=('vecs',))
    P.op('pool', lambda e: e.memset(tmpc_f, 1.0), writes=('tmpc',))
    P.op('pool', lambda e: e.affine_select(out=ident_f, in_=tmpc_f, pattern=[[-1, 128]], compare_op=ALU.is_equal,
                                           fill=0.0, base=0, channel_multiplier=1), reads=('tmpc',), writes=('ident_f',))
    P.op('pool', lambda e: e.tensor_copy(out=ident_b, in_=ident_f), reads=('ident_f',), writes=('ident_b',))
    P.op('pool', lambda e: e.memset(ones_b, 1.0), writes=('ones_b',))
    P.op('pool', lambda e: e.memset(nones_b, -1.0), writes=('nones_b',))
    P.op('pool', lambda e: e.affine_select(out=mask_b, in_=ones_b, pattern=[[1, 128]], compare_op=ALU.is_gt,
                                           fill=0.0, base=0, channel_multiplier=-1), reads=('ones_b',), writes=('mask_b',))
    P.op('pool', lambda e: e.affine_select(out=ntri_b, in_=nones_b, pattern=[[-1, 128]], compare_op=ALU.is_ge,
                                           fill=0.0, base=0, channel_multiplier=1), reads=('nones_b',), writes=('ntri_b',))
    P.op('dve', lambda e: e.tensor_scalar(out=gqs, in0=vecs[:, V_GQ:V_GQ + 2], scalar1=float(128 ** -0.5), scalar2=None,
                                          op0=ALU.mult), reads=('vecs',), writes=('gqs',))
    CONSTS = ('vecs', 'gqs', 'ident_f', 'ident_b', 'ones_b', 'nones_b', 'ntri_b', 'mask_b')

    def cast_w(name):
        w = W[name]
        wb = WB[name]
        key = ('wb', name)
        if name.startswith('wpool'):
            for g in range(4):
                for mo in range(4):
                    src = w[g][:, mo * 128:(mo + 1) * 128].rearrange("(kc p) j -> p kc j", p=128)
                    P.dma('pool', wb[g * 4 + mo], src, 'S_c_' + name)
        else:
            MC = wb.shape[0]
            for m in range(MC):
                src = w[:, m * 128:(m + 1) * 128].rearrange("(kc p) j -> p kc j", p=128)
                P.dma('pool', wb[m], src, 'S_c_' + name)
        P.res[key] = [('S_c_' + name, P.count['S_c_' + name]), {}]
        return key

    cast_keys = {}

    def need_w(name):
        if name not in cast_keys:
            cast_keys[name] = cast_w(name)
        return cast_keys[name]

    def layer_weights(kind, li):
        names = []
        if kind == 'sb':
            names += ["wqkv%d" % li, "wo%d" % li]
        elif kind == 'pool':
            names += ["wpool%d" % li]
        elif kind == 'conv':
            names += ["wcin%d" % li, "wcout%d" % li]
        if with_ffn:
            names += ["wg%d" % li, "wu%d" % li, "wd%d" % li]
        return names

    pending = []

    def tile_hook():
        if pending:
            need_w(pending.pop(0))

    def rmsnorm(gcol, normalize=True):
        sq = act
        xi = XS['i']
        X = xTb[xi]
        for g4 in range(4):
            c0 = g4 * 4
            P.op('act', lambda e, c0=c0: e.activation(out=sq[:, c0:c0 + 4, :], in_=X[:, c0:c0 + 4, :], func=AF.Square),
                 reads=[('xT', xi, c) for c in range(c0, c0 + 4)], writes=[('act', c) for c in range(c0, c0 + 4)])

        def mm(e):
            ins = None
            for c in range(DC):
                ins = e.matmul(ps[6], lhsT=ones_b, rhs=sq[:, c, :], start=(c == 0), stop=(c == DC - 1))
            return ins
        P.op('pe', mm, reads=[('act', c) for c in range(DC)] + ['ones_b'], writes=[PSK[6]])
        P.op('act', lambda e: e.activation(out=lnt[:, 0, :], in_=ps[6], func=AF.Ln, scale=1.0 / D, bias=EPS),
             reads=[PSK[6]], writes=[('lnt', 0)])
        P.op('act', lambda e: e.activation(out=rstd[:, 0, :], in_=lnt[:, 0, :], func=AF.Exp, scale=-0.5),
             reads=[('lnt', 0)], writes=[('rstd', 0)])
        for c in range(DC if normalize else 0):
            P.op('dve', lambda e, c=c: e.scalar_tensor_tensor(out=hTd[:, c, :], in0=X[:, c, :],
                                                               scalar=vecs[:, gcol + c:gcol + c + 1], in1=rstd[:, 0, :],
                                                               op0=ALU.mult, op1=ALU.mult),
                 reads=[('xT', xi, c), ('rstd', 0), 'vecs'], writes=[('hT', c)])

    def load_x(tt, bi=None):
        if bi is None:
            bi = XS['i']
        xres_read_deps('sync', tt)
        P.dma('sync', xTb[bi], xres_v[:, :, tt * TT:(tt + 1) * TT], 'S_xld%d' % bi,
              reads=[('xres', tt)], writes=[('xT', bi, c) for c in range(DC)])

    def prefetch_x(tt):
        if tt < NTT:
            load_x(tt, tt % 2)

    def store_x(tt):
        xi = XS['i']
        P.dma('act', xres_v[:, :, tt * TT:(tt + 1) * TT], xTb[xi], 'S_xst%d' % xi,
              reads=[('xT', xi, c) for c in range(DC)], writes=[('xres', tt)])

    def proj_chunk(pbank, wkey, wview, kc_n, rhs_fn, rhs_keys):
        def mm(e):
            ins = None
            for k in range(kc_n):
                ins = e.matmul(ps[pbank], lhsT=wview[:, k, :], rhs=rhs_fn(k), start=(k == 0), stop=(k == kc_n - 1))
            return ins
        return P.op('pe', mm, reads=[wkey] + list(rhs_keys), writes=[PSK[pbank]])

    def ffn_h(li, m):
        if not with_ffn:
            return
        xi = XS['i']
        X = xTb[xi]
        gcol = V_NFFN + li * 16
        P.op('dve', lambda e: e.tensor_scalar(out=hTd[:, m, :], in0=X[:, m, :], scalar1=vecs[:, gcol + m:gcol + m + 1],
                                              scalar2=None, op0=ALU.mult),
             reads=[('xT', xi, m), 'vecs'], writes=[('hT', m)])

    def ffn(li, nxt=None, h_done=False, extra=None):
        if not with_ffn:
            if nxt is not None:
                prefetch_x(nxt)
            if extra:
                extra()
            return
        xi = XS['i']
        X = xTb[xi]
        gcol = V_NFFN + li * 16
        if not h_done:
            for m in range(DC):
                ffn_h(li, m)
        rmsnorm(gcol, normalize=False)
        tg = lnt[:, 1, :]
        tu = rstd[:, 1, :]
        kg, ku, kd = need_w("wg%d" % li), need_w("wu%d" % li), need_w("wd%d" % li)
        wbg, wbu, wbd = WB["wg%d" % li], WB["wu%d" % li], WB["wd%d" % li]
        hkeys = [('hT', c) for c in range(DC)]
        for f in range(FC):
            b = f % 2
            wk, wv = ringA.load(wbg[f], DC, kg)
            proj_chunk(0 + b, wk, wv, DC, lambda k: hTd[:, k, :], hkeys)
            wk, wv = ringA.load(wbu[f], DC, ku)
            proj_chunk(2 + b, wk, wv, DC, lambda k: hTd[:, k, :], hkeys)
            P.op('dve', lambda e, b=b: e.tensor_tensor(out=tg, in0=ps[0 + b], in1=rstd[:, 0, :], op=ALU.mult),
                 reads=[PSK[0 + b], ('rstd', 0)], writes=[('lnt', 1)])
            P.op('act', lambda e, b=b: e.activation(out=sgt[:, b, :], in_=tg, func=AF.Silu),
                 reads=[('lnt', 1)], writes=[('sgt', b)])
            P.op('dve', lambda e, b=b: e.tensor_tensor(out=tu, in0=ps[2 + b], in1=rstd[:, 0, :], op=ALU.mult),
                 reads=[PSK[2 + b], ('rstd', 0)], writes=[('rstd', 1)])
            P.op('dve', lambda e, b=b, f=f: e.tensor_tensor(out=act[:, f, :], in0=tu, in1=sgt[:, b, :], op=ALU.mult),
                 reads=[('sgt', b), ('rstd', 1)], writes=[('act', f)])
        if nxt is not None:
            prefetch_x(nxt)
        if extra:
            extra()
        akeys = [('act', f) for f in range(FC)]
        for m in range(DC):
            b = 4 + (m % 2)
            wk, wv = ringB.load(wbd[m], FC, kd)
            proj_chunk(b, wk, wv, FC, lambda k: act[:, k, :], akeys)
            P.op('dve', lambda e, b=b, m=m: e.tensor_tensor(out=X[:, m, :], in0=X[:, m, :], in1=ps[b], op=ALU.add),
                 reads=[PSK[b]], writes=[('xT', xi, m)])

    def transpose_in():
        xin = misc.bitcast(F32).rearrange("p (b n) -> p b n", b=2)[:, :, 0:D]
        stg = big.bitcast(F32)[:, 0:2 * D].rearrange("p (b c t) -> p b c t", b=2, c=DC)
        for tb in range(NTB):
            b = tb % 2
            P.dma('sync', xin[:, b, :], x_in[tb * 128:(tb + 1) * 128, :], 'S_xin%d' % b, writes=[('xin', b)])
            for g in range(4):
                def tr(e, g=g, b=b):
                    ins = None
                    for j in range(4):
                        c = g * 4 + j
                        ins = e.transpose(out=ps[g][:, j * 128:(j + 1) * 128], in_=xin[:, b, c * 128:(c + 1) * 128],
                                          identity=ident_f)
                    return ins
                P.op('pe', tr, reads=[('xin', b), 'ident_f'], writes=[PSK[g]])
                eng = 'act' if g % 2 == 0 else 'dve'
                if eng == 'act':
                    P.op('act', lambda e, g=g, b=b: e.activation(out=stg[:, b, g * 4:(g + 1) * 4, :],
                                                                 in_=ps[g].rearrange("p (c t) -> p c t", c=4), func=AF.Copy),
                         reads=[PSK[g]], writes=[('stg', b, g)])
                else:
                    P.op('dve', lambda e, g=g, b=b: e.tensor_copy(out=stg[:, b, g * 4:(g + 1) * 4, :],
                                                                  in_=ps[g].rearrange("p (c t) -> p c t", c=4)),
                         reads=[PSK[g]], writes=[('stg', b, g)])
            P.dma('act', xres_v[:, :, tb * 128:(tb + 1) * 128], stg[:, b], 'S_stg%d' % b,
                  reads=[('stg', b, g) for g in range(4)], writes=[('xresb', tb)])
        for tt in range(NTT):
            toks = [P.res[('xresb', tt * 4 + j)][0] for j in range(4)]
            P.res[('xres', tt)] = [None, {}]
            for s, v in toks:
                P.res[('xres', tt)][1][s] = max(P.res[('xres', tt)][1].get(s, 0), v)

    def xres_read_deps(eng, tt):
        r = P.res.get(('xres', tt))
        if r:
            for s, v in r[1].items():
                P._wait(eng, s, v)

    def transpose_out():
        xo = misc.bitcast(F32).rearrange("p (b n) -> p b n", b=2)[:, :, 0:D]
        for tt in range(NTT):
            XS['i'] = tt % 2
            xi = XS['i']
            X = xTb[xi]
            load_x(tt)
            for j in range(4):
                tb = tt * 4 + j
                b = tb % 2
                for g in range(4):
                    def tr(e, g=g, j=j, X=X):
                        ins = None
                        for i in range(4):
                            c = g * 4 + i
                            ins = e.transpose(out=ps[g][:, i * 128:(i + 1) * 128], in_=X[:, c, j * 128:(j + 1) * 128],
                                              identity=ident_f)
                        return ins
                    P.op('pe', tr, reads=[('xT', xi, c) for c in range(g * 4, g * 4 + 4)] + ['ident_f'], writes=[PSK[g]])
                    if g % 2 == 0:
                        P.op('act', lambda e, g=g, b=b: e.activation(out=xo[:, b, g * 512:(g + 1) * 512], in_=ps[g], func=AF.Copy),
                             reads=[PSK[g]], writes=[('xo', b, g)])
                    else:
                        P.op('dve', lambda e, g=g, b=b: e.tensor_copy(out=xo[:, b, g * 512:(g + 1) * 512], in_=ps[g]),
                             reads=[PSK[g]], writes=[('xo', b, g)])
                P.dma('act', out[tb * 128:(tb + 1) * 128, :], xo[:, b, :], 'S_out%d' % b,
                      reads=[('xo', b, g) for g in range(4)], writes=[('outb', b)])
        for b in range(2):
            r = P.res.get(('outb', b))
            if r and r[0]:
                P.wait_tok('act', r[0])
                P.wait_tok('pool', r[0])
                P.wait_tok('sync', r[0])

    def sb_phase_a(li, j):
        kq = need_w("wqkv%d" % li)
        wb = WB["wqkv%d" % li]
        hkeys = [('hT', c) for c in range(DC)]
        qst = misc[:, 0:2 * TT].rearrange("p (b t) -> p b t", b=2)
        sqh = misc[:, 2 * TT:4 * TT].rearrange("p (b t) -> p b t", b=2)
        vst = misc[:, 4 * TT:4 * TT + 4 * D].rearrange("p (tb n) -> p tb n", tb=4)
        vT = sgt
        for tt in range(NTT):
            tile_hook()
            XS['i'] = tt % 2
            xi = XS['i']
            X = xTb[xi]
            if tt == 0:
                load_x(0, 0)
            rmsnorm(V_NMIX + li * 16)
            prefetch_x(tt + 1)
            it = 0
            deferred = []
            tails = []

            def qk_tail(which, hd, pb, b, dst):
                P.op('act', lambda e: e.activation(out=sqh[:, b, :], in_=ps[pb], func=AF.Square),
                     reads=[PSK[pb]], writes=[('sqh', b)])
                P.op('pe', lambda e: e.matmul(ps[4 + b], lhsT=ones_b, rhs=sqh[:, b, :], start=True, stop=True),
                     reads=[('sqh', b), 'ones_b'], writes=[PSK[4 + b]])
                P.op('act', lambda e: e.activation(out=lnt[:, b, :], in_=ps[4 + b], func=AF.Ln, scale=1.0 / 128, bias=EPS),
                     reads=[PSK[4 + b]], writes=[('lnt', b)])
                P.op('act', lambda e: e.activation(out=rstd[:, b, :], in_=lnt[:, b, :], func=AF.Exp, scale=-0.5),
                     reads=[('lnt', b)], writes=[('rstd', b)])
                gsc = gqs[:, j:j + 1] if which == 0 else vecs[:, V_GK + j:V_GK + j + 1]
                P.op('dve', lambda e: e.scalar_tensor_tensor(out=qst[:, b, :], in0=ps[pb], scalar=gsc,
                                                             in1=rstd[:, b, :], op0=ALU.mult, op1=ALU.mult),
                     reads=[PSK[pb], ('rstd', b), 'gqs', 'vecs'], writes=[('qst', b)])
                if deferred:
                    deferred.pop()()
                deferred.append(lambda: P.dma('act', dst[hd][:, tt * TT:(tt + 1) * TT], qst[:, b, :], 'S_qst%d' % b,
                                              reads=[('qst', b)], writes=[('qkd', which, hd, tt)]))

            def v_tail(hd, pb, b):
                pvb = ps[6 + b].bitcast(BF16)
                P.op('act', lambda e: e.activation(out=vT[:, b, :], in_=ps[pb], func=AF.Copy),
                     reads=[PSK[pb]], writes=[('vT', b)])

                def tr(e):
                    ins = None
                    for tb in range(4):
                        ins = e.transpose(out=pvb[:, tb * 128:(tb + 1) * 128], in_=vT[:, b, tb * 128:(tb + 1) * 128],
                                          identity=ident_b)
                    return ins
                P.op('pe', tr, reads=[('vT', b), 'ident_b'], writes=[PSK[6 + b]])
                P.op('dve', lambda e: e.tensor_copy(out=vst[:, :, hd * 128:(hd + 1) * 128],
                                                    in_=pvb[:, 0:512].rearrange("p (tb d) -> p tb d", tb=4)),
                     reads=[PSK[6 + b]], writes=[('vst', hd)])

            for which in range(3):
                dst = qT_d if which == 0 else kT_d
                for hd in range(NH):
                    pb = it % 4
                    b = it % 2
                    it += 1
                    wk, wv = ringA.load(wb[which * NH + hd], DC, kq)
                    proj_chunk(pb, wk, wv, DC, lambda k: hTd[:, k, :], hkeys)
                    if tails:
                        tails.pop()()
                    if which < 2:
                        tails.append(lambda which=which, hd=hd, pb=pb, b=b, dst=dst: qk_tail(which, hd, pb, b, dst))
                    else:
                        tails.append(lambda hd=hd, pb=pb, b=b: v_tail(hd, pb, b))
            tails.pop()()
            if deferred:
                deferred.pop()()
            P.dma('act', v_d[tt * 4:(tt + 1) * 4].rearrange("tb p n -> p tb n"), vst, 'S_vst',
                  reads=[('vst', hd) for hd in range(NH)], writes=[('vd', tt)])

    def sb_phase_b(li):
        HB = 3 * S
        assert HB <= 2 * DC * TT and HB <= FC * TT
        hb = [big[:, 0:HB], xTb[1].rearrange("p c t -> p (c t)").bitcast(BF16)[:, 0:HB]]
        sp2 = misc[:, 0:3072].rearrange("p (b t) -> p b t", b=3)
        at2 = misc[:, 3072:6144].rearrange("p (b t) -> p b t", b=3)
        NSL = 6
        srun = misc[:, 6144:6144 + NSL * TT].rearrange("p (b t) -> p b t", b=NSL)
        ost = misc[:, 6144 + NSL * TT:6144 + (NSL + 2) * TT].rearrange("p (b t) -> p b t", b=2)
        et2 = lnt.rearrange("p b t -> p (b t)")
        pZ = pall[:, 0:1024]
        pR = pall[:, 1024:2048]
        units = []
        for hd in range(NSEQ * NH):
            for qi in range(SNT):
                for kb in range(4 * qi + 3, 4 * qi - 1, -1):
                    units.append((hd, qi, [kb]))
                for kb in range(4 * qi - 1, 0, -2):
                    units.append((hd, qi, [kb, kb - 1]))
        NU = len(units)
        loaded = set()
        head_start = {}
        for ii, u in enumerate(units):
            head_start.setdefault(u[0], ii)
        info = [dict() for _ in range(NU)]
        state = {'slot': 0}

        def load_head(hd):
            if hd in loaded or hd >= NSEQ * NH:
                return
            loaded.add(hd)
            sq_, h_ = hd // NH, hd % NH
            t0_ = sq_ * S
            hbuf = hb[hd % 2]
            key = ('hb', hd % 2)
            sem = 'S_hb%d' % (hd % 2)
            P.dma('sync', hbuf[:, 0:S], qT_d[h_][:, t0_:t0_ + S], sem, writes=[key])
            P.op('sync', lambda e: e.dma_start(out=hbuf[:, S:2 * S], in_=kT_d[h_][:, t0_:t0_ + S]), sem=sem, inc=16)
            tok = P.op('sync', lambda e: e.dma_start(
                out=hbuf[:, 2 * S:3 * S].rearrange("p (tb d) -> p tb d", d=128),
                in_=v_d[t0_ // 128:(t0_ + S) // 128, :, h_ * 128:(h_ + 1) * 128].rearrange("tb p d -> p tb d")),
                sem=sem, inc=16)
            P.res[key] = [tok, {}]

        def geom(i):
            hd, qi, kbs = units[i]
            m = kbs[0] - 4 * qi
            c0 = 128 * m if m > 0 else 0
            return hd, qi, kbs, m, c0, TT - c0

        def stage1(i):
            hd, qi, kbs, m, c0, N = geom(i)
            if i == 0:
                load_head(0)
            if i == head_start[hd] + 3:
                load_head(hd + 1)
            hbuf = hb[hd % 2]
            hk = ('hb', hd % 2)
            W = N if len(kbs) == 1 else 2 * TT
            qs = hbuf[:, qi * TT + c0:(qi + 1) * TT]

            def mm(e):
                ins = None
                for si, kb in enumerate(kbs):
                    ins = e.matmul(pZ[:, si * TT:si * TT + N], lhsT=hbuf[:, S + kb * 128:S + (kb + 1) * 128], rhs=qs,
                                   start=True, stop=True)
                return ins
            P.op('pe', mm, reads=[hk], writes=['pz'])
            P.op('act', lambda e: e.activation(out=et2[:, 0:W], in_=pZ[:, 0:W], func=AF.Exp),
                 reads=['pz'], writes=['et'])
            sp = sp2[:, i % 3, :]
            P.op('act', lambda e: e.activation(out=sp[:, 0:W], in_=et2[:, 0:W], func=AF.Ln, bias=1.0),
                 reads=['et'], writes=[('sp', i % 3)])
            if m >= 0:
                P.op('dve', lambda e: e.tensor_tensor(out=sp[:, 0:128], in0=sp[:, 0:128], in1=mask_b, op=ALU.mult),
                     reads=['mask_b'], writes=[('sp', i % 3)])
            first = (kbs[0] == 4 * qi + 3)
            prev = []
            for si, kb in enumerate(kbs):
                old = state['slot']
                new = (old + 1) % NSL
                state['slot'] = new
                sn = srun[:, new, :]
                so = srun[:, old, :]
                if c0 > 0:
                    P.op('dve', lambda e, sn=sn: e.memset(sn[:, 0:c0], 0.0), writes=[('srun', new)])
                if first and si == 0:
                    prev.append(None)
                    P.op('dve', lambda e, sn=sn: e.tensor_copy(out=sn[:, c0:TT], in_=sp[:, 0:N]),
                         reads=[('sp', i % 3)], writes=[('srun', new)])
                else:
                    prev.append(old)
                    P.op('dve', lambda e, sn=sn, so=so, si=si: e.tensor_tensor(out=sn[:, c0:TT], in0=so[:, c0:TT],
                                                                              in1=sp[:, si * TT:si * TT + N], op=ALU.add),
                         reads=[('sp', i % 3), ('srun', old)], writes=[('srun', new)])
            info[i]['prev'] = prev
            info[i]['first'] = first

        def stage2(i):
            hd, qi, kbs, m, c0, N = geom(i)
            hbuf = hb[hd % 2]
            hk = ('hb', hd % 2)
            W = N if len(kbs) == 1 else 2 * TT
            qs = hbuf[:, qi * TT + c0:(qi + 1) * TT]
            sp = sp2[:, i % 3, :]
            prev = info[i]['prev']

            def mm(e):
                ins = None
                for si, kb in enumerate(kbs):
                    o = pR[:, si * TT:si * TT + N]
                    e.matmul(o, lhsT=ntri_b, rhs=sp[:, si * TT:si * TT + N], start=True, stop=False)
                    if prev[si] is not None:
                        e.matmul(o, lhsT=nones_b, rhs=srun[:, prev[si], c0:TT], start=False, stop=False)
                    ins = e.matmul(o, lhsT=hbuf[:, S + kb * 128:S + (kb + 1) * 128], rhs=qs, start=False, stop=True)
                return ins
            rd = [hk, ('sp', i % 3), 'ntri_b', 'nones_b'] + [('srun', p) for p in prev if p is not None]
            P.op('pe', mm, reads=rd, writes=['pr'])
            at = at2[:, i % 3, :]
            P.op('act', lambda e: e.activation(out=at[:, 0:W], in_=pR[:, 0:W], func=AF.Exp),
                 reads=['pr'], writes=[('at', i % 3)])
            if m >= 0:
                P.op('dve', lambda e: e.tensor_tensor(out=at[:, 0:128], in0=at[:, 0:128], in1=mask_b, op=ALU.mult),
                     reads=['mask_b'], writes=[('at', i % 3)])

        def stage3(i):
            hd, qi, kbs, m, c0, N = geom(i)
            hbuf = hb[hd % 2]
            hk = ('hb', hd % 2)
            ob = 4 + (hd * SNT + qi) % 2
            first = info[i]['first']
            at = at2[:, i % 3, :]

            def mm(e):
                ins = None
                for si, kb in enumerate(kbs):
                    ins = e.matmul(ps[ob][:, c0:TT], lhsT=hbuf[:, 2 * S + kb * 128:2 * S + (kb + 1) * 128],
                                   rhs=at[:, si * TT:si * TT + N], start=(first and si == 0), stop=(kb == 0),
                                   skip_group_check=True)
                return ins
            P.op('pe', mm, reads=[hk, ('at', i % 3)], writes=[PSK[ob]])
            if kbs[-1] == 0:
                sbuf = (hd * SNT + qi) % 2
                gt0 = (hd // NH) * S + qi * TT
                P.op('dve', lambda e: e.tensor_copy(out=ost[:, sbuf, :], in_=ps[ob]),
                     reads=[PSK[ob]], writes=[('ost', sbuf)])
                P.dma('act', oT_d[hd % NH][:, gt0:gt0 + TT], ost[:, sbuf, :], 'S_ost%d' % sbuf,
                      reads=[('ost', sbuf)], writes=[('od', hd, qi)])

        for n in range(NU + 2):
            if n < NU:
                stage1(n)
            if 0 <= n - 1 < NU:
                stage2(n - 1)
            if 0 <= n - 2 < NU:
                stage3(n - 2)

    def sb_phase_c(li):
        ko = need_w("wo%d" % li)
        wb = WB["wo%d" % li]
        oT = misc[:, 0:NH * TT].rearrange("p (h t) -> p h t", h=NH)
        okeys = [('oT', h) for h in range(NH)]

        def load_oT(tt):
            if tt < NTT:
                P.dma('sync', oT, oT_d[:, :, tt * TT:(tt + 1) * TT].rearrange("h p t -> p h t"), 'S_old', writes=okeys)

        for tt in range(NTT):
            tile_hook()
            XS['i'] = tt % 2
            xi = XS['i']
            X = xTb[xi]
            if tt == 0:
                load_x(0, 0)
                load_oT(0)
            for m in range(DC):
                b = m % 2
                wk, wv = ringA.load(wb[m], DC, ko)
                proj_chunk(b, wk, wv, DC, lambda k: oT[:, k, :], okeys)
                P.op('dve', lambda e, b=b, m=m, X=X: e.tensor_tensor(out=X[:, m, :], in0=X[:, m, :], in1=ps[b], op=ALU.add),
                     reads=[PSK[b]], writes=[('xT', xi, m)])
                ffn_h(li, m)
            ffn(li, tt + 1, h_done=True, extra=lambda tt=tt: load_oT(tt + 1))
            store_x(tt)

    def pool_layer(li):
        kp = need_w("wpool%d" % li)
        wb = WB["wpool%d" % li]
        pT = act
        U = 16 + TT
        wa = misc.bitcast(F32)[:, 0:4 * U].rearrange("p (c u) -> p c u", c=4)
        wb2 = misc.bitcast(F32)[:, 4 * U:8 * U].rearrange("p (c u) -> p c u", c=4)
        for tt in range(NTT):
            tile_hook()
            XS['i'] = tt % 2
            xi = XS['i']
            X = xTb[xi]
            if tt == 0:
                load_x(0, 0)
            if tt % SNT == 0:
                P.op('dve', lambda e: e.memset(hT[:, :, 0:16], 0.0), writes=[('hThalo',)])
            rmsnorm(V_NMIX + li * 16)
            for g in range(4):
                w = POOL_W[g]
                cs = slice(g * 4, g * 4 + 4)
                hk = [('hT', c) for c in range(g * 4, g * 4 + 4)] + [('hThalo',)]
                P.op('dve', lambda e, cs=cs: e.tensor_tensor(out=wa[:, :, 1:U], in0=hT[:, cs, 1:U], in1=hT[:, cs, 0:U - 1], op=ALU.add),
                     reads=hk, writes=['wa'])
                cur, oth, curk, othk = wa, wb2, 'wa', 'wb'
                sh = 2
                lo = 1
                while sh < w:
                    lo2 = lo + sh
                    P.op('dve', lambda e, cur=cur, oth=oth, lo2=lo2, sh=sh: e.tensor_tensor(
                        out=oth[:, :, lo2:U], in0=cur[:, :, lo2:U], in1=cur[:, :, lo2 - sh:U - sh], op=ALU.add),
                        reads=[curk], writes=[othk])
                    cur, oth, curk, othk = oth, cur, othk, curk
                    lo = lo2
                    sh *= 2
                P.op('dve', lambda e, cur=cur, cs=cs, w=w: e.scalar_tensor_tensor(
                    out=pT[:, cs, :], in0=cur[:, :, 16:U], scalar=1.0 / w, in1=hT[:, cs, 16:U], op0=ALU.mult, op1=ALU.subtract),
                    reads=[curk] + hk, writes=[('act', c) for c in range(g * 4, g * 4 + 4)])
                if tt % SNT == 0:
                    nfix = w - 1
                    for c in range(4):
                        P.op('dve', lambda e, cur=cur, c=c, g=g, nfix=nfix: e.tensor_tensor(
                            out=cur[:, c, 16:16 + nfix], in0=cur[:, c, 16:16 + nfix],
                            in1=vecs[:, V_INVCNT + g * 16:V_INVCNT + g * 16 + nfix], op=ALU.mult),
                            reads=[curk, 'vecs'], writes=[curk])
                        P.op('dve', lambda e, cur=cur, c=c, g=g, nfix=nfix: e.tensor_tensor(
                            out=pT[:, g * 4 + c, 0:nfix], in0=cur[:, c, 16:16 + nfix], in1=hT[:, g * 4 + c, 16:16 + nfix],
                            op=ALU.subtract),
                            reads=[curk] + hk, writes=[('act', g * 4 + c)])
                for mo in range(4):
                    c = g * 4 + mo
                    b = c % 2
                    wk, wv = ringA.load(wb[c], 4, kp)
                    proj_chunk(b, wk, wv, 4, lambda k, g=g: pT[:, g * 4 + k, :], [('act', g * 4 + k) for k in range(4)])
                    P.op('dve', lambda e, b=b, c=c, X=X: e.scalar_tensor_tensor(
                        out=X[:, c, :], in0=ps[b], scalar=vecs[:, V_PSCALE + c:V_PSCALE + c + 1], in1=X[:, c, :],
                        op0=ALU.mult, op1=ALU.add),
                        reads=[PSK[b], 'vecs'], writes=[('xT', xi, c)])
            if (tt + 1) % SNT != 0:
                P.op('act', lambda e: e.activation(out=hT[:, :, 0:16], in_=hT[:, :, TT:TT + 16], func=AF.Copy),
                     reads=[('hT', c) for c in range(DC)], writes=[('hThalo',)])
            ffn(li, tt + 1)
            if tt + 1 < NTT:
                pass
            store_x(tt)

    def conv_layer(li):
        kci, kco = need_w("wcin%d" % li), need_w("wcout%d" % li)
        wbi, wbo = WB["wcin%d" % li], WB["wcout%d" % li]
        byT = act
        mf = misc.bitcast(F32)
        gb = mf[:, 0:2 * 516].rearrange("p (b u) -> p b u", b=2)
        ub = mf[:, 1032:1032 + 2 * TT].rearrange("p (b t) -> p b t", b=2)
        yb = mf[:, 2056:2056 + 2 * TT].rearrange("p (b t) -> p b t", b=2)
        gh = mf[:, 3080:3080 + 2 * DC].rearrange("p (c u) -> p c u", u=2)
        hkeys = [('hT', c) for c in range(DC)]
        for tt in range(NTT):
            tile_hook()
            XS['i'] = tt % 2
            xi = XS['i']
            X = xTb[xi]
            if tt == 0:
                load_x(0, 0)
            rmsnorm(V_NMIX + li * 16)
            if tt % SNT == 0:
                P.op('dve', lambda e: e.memset(gh, 0.0), writes=[('gh', c) for c in range(DC)])
            for m in range(DC):
                b = m % 2
                wk, wv = ringA.load(wbi[m], DC, kci)
                proj_chunk(0 + b, wk, wv, DC, lambda k: hTd[:, k, :], hkeys)
                wk, wv = ringA.load(wbi[DC + m], DC, kci)
                proj_chunk(2 + b, wk, wv, DC, lambda k: hTd[:, k, :], hkeys)
                wk, wv = ringA.load(wbi[2 * DC + m], DC, kci)
                proj_chunk(4 + b, wk, wv, DC, lambda k: hTd[:, k, :], hkeys)
                P.op('act', lambda e, b=b: e.activation(out=ub[:, b, :], in_=ps[4 + b], func=AF.Copy),
                     reads=[PSK[4 + b]], writes=[('ub', b)])
                P.op('dve', lambda e, b=b, m=m: e.tensor_copy(out=gb[:, b, 0:2], in_=gh[:, m, :]),
                     reads=[('gh', m)], writes=[('gb', b)])
                P.op('dve', lambda e, b=b: e.tensor_tensor(out=gb[:, b, 2:2 + TT], in0=ps[2 + b], in1=ub[:, b, :], op=ALU.mult),
                     reads=[PSK[2 + b], ('ub', b)], writes=[('gb', b)])
                P.op('dve', lambda e, b=b, m=m: e.tensor_copy(out=gh[:, m, :], in_=gb[:, b, TT:TT + 2]),
                     reads=[('gb', b)], writes=[('gh', m)])
                cw = lambda jj, m=m: vecs[:, V_CONVW + jj * 16 + m:V_CONVW + jj * 16 + m + 1]
                P.op('dve', lambda e, b=b, cw=cw: e.tensor_scalar(out=yb[:, b, :], in0=gb[:, b, 2:2 + TT], scalar1=cw(2), scalar2=None,
                                                                  op0=ALU.mult),
                     reads=[('gb', b), 'vecs'], writes=[('yb', b)])
                P.op('dve', lambda e, b=b, cw=cw: e.scalar_tensor_tensor(out=yb[:, b, :], in0=gb[:, b, 1:1 + TT], scalar=cw(1),
                                                                         in1=yb[:, b, :], op0=ALU.mult, op1=ALU.add),
                     reads=[('gb', b), 'vecs'], writes=[('yb', b)])
                P.op('dve', lambda e, b=b, cw=cw: e.scalar_tensor_tensor(out=yb[:, b, :], in0=gb[:, b, 0:TT], scalar=cw(0),
                                                                         in1=yb[:, b, :], op0=ALU.mult, op1=ALU.add),
                     reads=[('gb', b), 'vecs'], writes=[('yb', b)])
                P.op('dve', lambda e, b=b, m=m: e.tensor_tensor(out=byT[:, m, :], in0=ps[0 + b], in1=yb[:, b, :], op=ALU.mult),
                     reads=[PSK[0 + b], ('yb', b)], writes=[('act', m)])
            bkeys = [('act', m) for m in range(DC)]
            for mo in range(DC):
                b = 6 + mo % 2
                wk, wv = ringA.load(wbo[mo], DC, kco)
                proj_chunk(b, wk, wv, DC, lambda k: byT[:, k, :], bkeys)
                P.op('dve', lambda e, b=b, mo=mo, X=X: e.tensor_tensor(out=X[:, mo, :], in0=X[:, mo, :], in1=ps[b], op=ALU.add),
                     reads=[PSK[b]], writes=[('xT', xi, mo)])
                ffn_h(li, mo)
            ffn(li, tt + 1, h_done=True)
            store_x(tt)

    def ffn_only_layer(li):
        for tt in range(NTT):
            tile_hook()
            XS['i'] = tt % 2
            xi = XS['i']
            X = xTb[xi]
            if tt == 0:
                load_x(0, 0)
            ffn(li, tt + 1)
            store_x(tt)

    first = layer_weights(*layers[0])
    need_w(first[0])
    transpose_in()
    for nm in first[1:]:
        need_w(nm)
    P.barrier()
    for idx, (kind, li) in enumerate(layers):
        j = li // 3
        if idx + 1 < len(layers):
            pending.extend(layer_weights(*layers[idx + 1]))
        if kind == 'sb':
            sb_phase_a(li, j)
            P.barrier()
            sb_phase_b(li)
            P.barrier()
            sb_phase_c(li)
        elif kind == 'pool':
            pool_layer(li)
        elif kind == 'conv':
            conv_layer(li)
        elif kind == 'ffn':
            ffn_only_layer(li)
        while pending:
            need_w(pending.pop(0))
        P.barrier()
    transpose_out()
    P.replay()
    return nc


LAYERS = [('sb', 0), ('pool', 1), ('conv', 2), ('sb', 3)]


def make_vecs(norm_mix_g, norm_ffn_g, sb_g_q, sb_g_k, pool_scale, conv_w):
    v = np.zeros((128, NV), np.float32)
    fm = lambda a: np.asarray(a, np.float32).reshape(DC, 128).T
    for l in range(DEPTH):
        v[:, V_NMIX + l * 16:V_NMIX + (l + 1) * 16] = fm(norm_mix_g[l])
        v[:, V_NFFN + l * 16:V_NFFN + (l + 1) * 16] = fm(norm_ffn_g[l])
    v[:, V_PSCALE:V_PSCALE + 16] = fm(pool_scale[0])
    for jj in range(3):
        v[:, V_CONVW + jj * 16:V_CONVW + (jj + 1) * 16] = fm(conv_w[0][jj])
    for j in range(2):
        v[:, V_GQ + j] = np.asarray(sb_g_q[j], np.float32)
        v[:, V_GK + j] = np.asarray(sb_g_k[j], np.float32)
    for g, w in enumerate(POOL_W):
        v[:, V_INVCNT + g * 16:V_INVCNT + (g + 1) * 16] = (1.0 / np.minimum(np.arange(1, 17), w)).astype(np.float32)[None, :]
    return v


def make_in_maps(T, layers, x_shards, vecs, weights, with_ffn=True):
    maps = []
    for xs in x_shards:
        m = {"x": np.ascontiguousarray(xs, dtype=np.float32), "vecs": vecs}
        for kind, li in layers:
            j = li // 3
            if kind == 'sb':
                m["wqkv%d" % li] = weights["sb_w_qkv"][j]
                m["wo%d" % li] = weights["sb_w_o"][j]
            elif kind == 'pool':
                m["wpool%d" % li] = weights["pool_w"][0]
            elif kind == 'conv':
                m["wcin%d" % li] = weights["conv_w_in"][0]
                m["wcout%d" % li] = weights["conv_w_out"][0]
            if with_ffn:
                m["wg%d" % li] = weights["ffn_w_gate"][li]
                m["wu%d" % li] = weights["ffn_w_up"][li]
                m["wd%d" % li] = weights["ffn_w_down"][li]
        maps.append(m)
    return maps


def kernel(x, norm_mix_g, norm_ffn_g, sb_w_qkv, sb_g_q, sb_g_k, sb_w_o, pool_w, pool_scale,
           conv_w_in, conv_w, conv_w_out, ffn_w_gate, ffn_w_up, ffn_w_down):
    x = np.asarray(x, np.float32)
    B, S, _ = x.shape
    weights = dict(sb_w_qkv=np.asarray(sb_w_qkv, np.float32), sb_w_o=np.asarray(sb_w_o, np.float32),
                   pool_w=np.asarray(pool_w, np.float32), conv_w_in=np.asarray(conv_w_in, np.float32),
                   conv_w_out=np.asarray(conv_w_out, np.float32), ffn_w_gate=np.asarray(ffn_w_gate, np.float32),
                   ffn_w_up=np.asarray(ffn_w_up, np.float32), ffn_w_down=np.asarray(ffn_w_down, np.float32))
    vecs = make_vecs(np.asarray(norm_mix_g), np.asarray(norm_ffn_g), np.asarray(sb_g_q), np.asarray(sb_g_k),
                     np.asarray(pool_scale), np.asarray(conv_w))
    nc = build(S, LAYERS, NSEQ=1)
    work = [0, 1, 4, 5][:B]
    real = make_in_maps(S, LAYERS, [x[b] for b in range(B)], vecs, weights)
    zero = {k: np.zeros_like(v) for k, v in real[0].items()}
    in_maps = [zero] * N_CORES
    in_maps = list(in_maps)
    for b, c in enumerate(work):
        in_maps[c] = real[b]
    res = run_bass_kernel_spmd(nc, in_maps, core_ids=list(range(N_CORES)))
    return np.stack([np.asarray(res.results[c]["out"], np.float32) for c in work], axis=0)
```
